# Optimizing a Trainium2 kernel written in Bass

```python
import math
import jax, jax.numpy as jnp
from jax import lax
import numpy as np

D_MODEL = 2048
BATCH = 4
SEQ = 4096
DEPTH = 2

LRU_WIDTH = D_MODEL
LRU_HEADS = 16
LRU_HEAD_DIM = LRU_WIDTH // LRU_HEADS
CONV_W = 4
LRU_C = 8.0
HEAD_DIM = 128
N_Q_HEADS = 16
N_KV_HEADS = 4
GQA_GROUP = N_Q_HEADS // N_KV_HEADS
WINDOW = 128
ATTN_WIDTH = N_Q_HEADS * HEAD_DIM
KV_WIDTH = N_KV_HEADS * HEAD_DIM
N_BRANCHES = 2
C_IN = 2 * LRU_WIDTH + ATTN_WIDTH + 2 * KV_WIDTH + N_BRANCHES * D_MODEL
SPLITS = tuple(np.cumsum([LRU_WIDTH, LRU_WIDTH, ATTN_WIDTH, KV_WIDTH, KV_WIDTH]).tolist())
FFN_HIDDEN = int(math.ceil(8 * D_MODEL / 3 / 256) * 256)
N_MOD = 6
EPS = 1e-6

kernel_name = "hybrid_rglru_swa_sink_adaln_block"


def rms_norm(x, g):
    xf = x.astype(jnp.float32)
    inv = lax.rsqrt(jnp.mean(xf * xf, axis=-1, keepdims=True) + EPS)
    return (xf * inv).astype(x.dtype) * g


def modulate(h, shift, scale):
    return h * (1 + scale) + shift


def causal_depthwise_conv(u, w, b):
    S = u.shape[1]
    up = jnp.pad(u, ((0, 0), (CONV_W - 1, 0), (0, 0)))
    out = b
    for k in range(CONV_W):
        out = out + up[:, k:k + S, :] * w[k]
    return out


def rg_lru(u, wa, ba, wx, bx, lam):
    B, S, C = u.shape
    uh = u.reshape(B, S, LRU_HEADS, LRU_HEAD_DIM)
    r = jax.nn.sigmoid(jnp.einsum('bshi,hij->bshj', uh, wa).reshape(B, S, C) + ba)
    i = jax.nn.sigmoid(jnp.einsum('bshi,hij->bshj', uh, wx).reshape(B, S, C) + bx)
    log_a = -LRU_C * r.astype(jnp.float32) * jax.nn.softplus(-lam.astype(jnp.float32))
    a = jnp.exp(log_a)
    beta = jnp.sqrt(-jnp.expm1(2.0 * log_a))
    inp = beta * (i * u).astype(jnp.float32)

    def combine(left, right):
        a1, b1 = left
        a2, b2 = right
        return a1 * a2, a2 * b1 + b2

    _, h = lax.associative_scan(combine, (a, inp), axis=1)
    return h.astype(u.dtype)


def sliding_window_attention(q, k, v, sinks):
    B, S = q.shape[0], q.shape[1]
    nb = S // WINDOW
    qb = q.reshape(B, nb, WINDOW, N_KV_HEADS, GQA_GROUP, HEAD_DIM)
    kb = k.reshape(B, nb, WINDOW, N_KV_HEADS, HEAD_DIM)
    vb = v.reshape(B, nb, WINDOW, N_KV_HEADS, HEAD_DIM)
    pad = ((0, 0), (1, 0), (0, 0), (0, 0), (0, 0))
    kw = jnp.concatenate([jnp.pad(kb, pad)[:, :-1], kb], axis=2)
    vw = jnp.concatenate([jnp.pad(vb, pad)[:, :-1], vb], axis=2)
    s = jnp.einsum('bnqhgd,bnkhd->bnhgqk', qb, kw).astype(jnp.float32) * (HEAD_DIM ** -0.5)
    qi = jnp.arange(WINDOW)[:, None]
    kk = jnp.arange(2 * WINDOW)[None, :]
    band = (kk > qi) & (kk <= qi + WINDOW)
    blk = jnp.arange(nb)[:, None, None]
    valid = band[None] & (blk * WINDOW + kk[None] - WINDOW >= 0)
    s = jnp.where(valid[None, :, None, None], s, -jnp.inf)
    sink = sinks.astype(jnp.float32).reshape(N_KV_HEADS, GQA_GROUP)[None, None, :, :, None, None]
    m = jnp.maximum(jnp.max(s, axis=-1, keepdims=True), sink)
    p = jnp.exp(s - m)
    denom = jnp.sum(p, axis=-1, keepdims=True) + jnp.exp(sink - m)
    p = (p / denom).astype(v.dtype)
    o = jnp.einsum('bnhgqk,bnkhd->bnqhgd', p, vw)
    return o.reshape(B, S, ATTN_WIDTH)


def setup_inputs(seed: int = 0) -> dict:
    key = jax.random.key(seed)
    ks = jax.random.split(key, 24)
    f32 = jnp.float32

    def nrm(k, shape, scale):
        return jax.random.normal(k, shape, f32) * scale

    a0 = jax.random.uniform(ks[11], (DEPTH, LRU_WIDTH), f32, 0.9, 0.999)
    p0 = a0 ** (1.0 / LRU_C)
    lru_lambda = jnp.log(p0) - jnp.log1p(-p0)
    return {
        "x": nrm(ks[0], (BATCH, SEQ, D_MODEL), 1.0),
        "c": nrm(ks[1], (BATCH, D_MODEL), 1.0),
        "ada_w": nrm(ks[2], (DEPTH, D_MODEL, N_MOD * D_MODEL), D_MODEL ** -0.5),
        "ada_b": nrm(ks[3], (DEPTH, N_MOD * D_MODEL), 0.01),
        "norm1_g": 1.0 + nrm(ks[4], (DEPTH, D_MODEL), 0.02),
        "w_in": nrm(ks[5], (DEPTH, D_MODEL, C_IN), D_MODEL ** -0.5),
        "b_in": nrm(ks[6], (DEPTH, C_IN), 0.01),
        "conv_w": nrm(ks[7], (DEPTH, CONV_W, LRU_WIDTH), CONV_W ** -0.5),
        "conv_b": nrm(ks[8], (DEPTH, LRU_WIDTH), 0.01),
        "lru_wa": nrm(ks[9], (DEPTH, LRU_HEADS, LRU_HEAD_DIM, LRU_HEAD_DIM), LRU_HEAD_DIM ** -0.5),
        "lru_ba": nrm(ks[10], (DEPTH, LRU_WIDTH), 0.01),
        "lru_wx": nrm(ks[12], (DEPTH, LRU_HEADS, LRU_HEAD_DIM, LRU_HEAD_DIM), LRU_HEAD_DIM ** -0.5),
        "lru_bx": nrm(ks[13], (DEPTH, LRU_WIDTH), 0.01),
        "lru_lambda": lru_lambda,
        "sinks": nrm(ks[14], (DEPTH, N_Q_HEADS), 0.5),
        "w_lru_out": nrm(ks[15], (DEPTH, LRU_WIDTH, D_MODEL), LRU_WIDTH ** -0.5),
        "w_attn_out": nrm(ks[16], (DEPTH, ATTN_WIDTH, D_MODEL), ATTN_WIDTH ** -0.5),
        "w_o": nrm(ks[17], (DEPTH, D_MODEL, D_MODEL), D_MODEL ** -0.5),
        "norm2_g": 1.0 + nrm(ks[18], (DEPTH, D_MODEL), 0.02),
        "w_ffn_in": nrm(ks[19], (DEPTH, D_MODEL, 2 * FFN_HIDDEN), D_MODEL ** -0.5),
        "w_ffn_out": nrm(ks[20], (DEPTH, FFN_HIDDEN, D_MODEL), FFN_HIDDEN ** -0.5),
        "final_g": 1.0 + nrm(ks[21], (D_MODEL,), 0.02),
    }


def reference(x, c, ada_w, ada_b, norm1_g, w_in, b_in, conv_w, conv_b, lru_wa, lru_ba,
              lru_wx, lru_bx, lru_lambda, sinks, w_lru_out, w_attn_out, w_o, norm2_g,
              w_ffn_in, w_ffn_out, final_g):
    B, S, _ = x.shape
    c_act = jax.nn.silu(c)
    for l in range(DEPTH):
        mod = (c_act @ ada_w[l] + ada_b[l])[:, None, :]
        sh1, sc1, g1, sh2, sc2, g2 = jnp.split(mod, N_MOD, axis=-1)

        h = modulate(rms_norm(x, norm1_g[l]), sh1, sc1)
        proj = h @ w_in[l] + b_in[l]
        u, lru_gate, q, k, v, mix_gates = jnp.split(proj, SPLITS, axis=-1)

        u = causal_depthwise_conv(u, conv_w[l], conv_b[l])
        y_lru = rg_lru(u, lru_wa[l], lru_ba[l], lru_wx[l], lru_bx[l], lru_lambda[l])
        y_lru = y_lru * jax.nn.gelu(lru_gate, approximate=True)

        y_attn = sliding_window_attention(
            q.reshape(B, S, N_Q_HEADS, HEAD_DIM),
            k.reshape(B, S, N_KV_HEADS, HEAD_DIM),
            v.reshape(B, S, N_KV_HEADS, HEAD_DIM),
            sinks[l])

        gate_lru, gate_attn = jnp.split(jax.nn.sigmoid(mix_gates), N_BRANCHES, axis=-1)
        merged = gate_lru * (y_lru @ w_lru_out[l]) + gate_attn * (y_attn @ w_attn_out[l])
        x = x + g1 * (merged @ w_o[l])

        h2 = modulate(rms_norm(x, norm2_g[l]), sh2, sc2)
        gate, up = jnp.split(h2 @ w_ffn_in[l], 2, axis=-1)
        x = x + g2 * ((jax.nn.silu(gate) * up) @ w_ffn_out[l])
    return rms_norm(x, final_g)
```

```python
import math
from contextlib import ExitStack

import numpy as np
import concourse.bass as bass
import concourse.mybir as mybir
from concourse.bass_utils import run_bass_kernel_spmd

F32 = mybir.dt.float32
BF16 = mybir.dt.bfloat16
AF = mybir.ActivationFunctionType
ALU = mybir.AluOpType

EPS = 1e-6
LRU_C = 8.0


class Cfg:
    def __init__(self, D=2048, NQ=16, NKV=4, FH=5632, SEQ=4096, BATCH=4, DEPTH=2, T=1024, SPLIT=2):
        self.D, self.NQ, self.NKV, self.FH, self.SEQ, self.BATCH, self.DEPTH, self.T = D, NQ, NKV, FH, SEQ, BATCH, DEPTH, T
        self.SPLIT = SPLIT
        self.NCORES = BATCH * SPLIT
        self.KC = D // 128
        self.G = NQ // NKV
        self.AWC = NQ
        self.FC = FH // 128
        self.TOK = SEQ // SPLIT
        self.NT = self.TOK // T
        self.NS = T // 512
        self.NB = T // 128
        self.CIN = 2 * D + NQ * 128 + 2 * NKV * 128 + 2 * D
        self.MC = self.CIN // 128
        KC = self.KC
        self.U0, self.GT0, self.Q0 = 0, KC, 2 * KC
        self.K0 = self.Q0 + NQ
        self.V0 = self.K0 + NKV
        self.GA0 = self.V0 + NKV
        self.GB0 = self.GA0 + KC
        o = {}
        p = 0
        for name, n in [("ada_b", 6 * KC), ("n1g", KC), ("n2g", KC), ("b_in", self.MC), ("cw", 4 * KC), ("cb", KC),
                        ("ba", KC), ("bx", KC), ("lam", KC), ("sink", NQ), ("fg", KC)]:
            o[name] = p
            p += n
        self.VO, self.NV = o, p
        d = {}
        p = 0
        for name, n in [("mod", 6 * KC), ("s1", KC), ("s2", KC), ("nc8", KC), ("nc16", KC), ("hnc8", KC), ("hba", KC), ("hbx", KC), ("z", KC), ("t", KC), ("esk", NQ)]:
            d[name] = p
            p += n
        self.DO, self.NDV = d, p


class Sch:
    def __init__(self, nc, es):
        self.nc = nc
        self.es = es
        self.prog = {k: [] for k in ("pe", "act", "dve", "pool", "sp")}
        self.semh = {}
        self.cnt = {}
        for k in ("pe", "act", "dve", "pool"):
            self.semh[k] = es.enter_context(nc.semaphore("s_" + k))
            self.cnt[k] = 0
        self.clock = {k: {} for k in self.prog}
        self.snap = {}
        self.lastw = {}
        self.readers = {}
        self.nwait = 0

    def slot(self, name):
        if name not in self.semh:
            self.semh[name] = self.es.enter_context(self.nc.semaphore("d_" + name))
            self.cnt[name] = 0
        return name

    def _deps(self, eng, reads, writes):
        deps = {}

        def add(k):
            if k is None:
                return
            f, n = k
            if deps.get(f, 0) < n:
                deps[f] = n

        for r in reads:
            add(self.lastw.get(r))
            if r[0] == "ps":
                for f, n in self.readers.get(r, {}).items():
                    if f != eng:
                        add((f, n))
        for w in writes:
            add(self.lastw.get(w))
            for f, n in self.readers.get(w, {}).items():
                if f == eng:
                    continue
                add((f, n))
        return deps

    def _emit_waits(self, eng, deps):
        ck = self.clock[eng]
        for f, n in deps.items():
            if f == eng and eng == "pe":
                continue
            if ck.get(f, 0) >= n:
                continue
            assert n <= self.cnt[f], ("dependency on unmaterialised count", eng, f, n, self.cnt[f])
            h = self.semh[f]
            self.prog[eng].append(("w", h, n))
            self.nwait += 1
            sn = self.snap.get((f, n))
            if sn:
                for a, b in sn.items():
                    if ck.get(a, 0) < b:
                        ck[a] = b
            ck[f] = max(ck.get(f, 0), n)

    def _record(self, key, reads, writes):
        for r in reads:
            self.readers.setdefault(r, {})[key[0]] = key[1]
        for w in writes:
            self.lastw[w] = key
            self.readers[w] = {}

    def op(self, eng, fn, reads=(), writes=(), inc=True):
        deps = self._deps(eng, reads, writes)
        self._emit_waits(eng, deps)
        n = self.cnt[eng] + 1
        if inc:
            self.cnt[eng] = n
            self.prog[eng].append(("i", fn, self.semh[eng], 1))
            sn = dict(self.clock[eng])
            sn[eng] = n
            self.snap[(eng, n)] = sn
            self.clock[eng][eng] = max(self.clock[eng].get(eng, 0), 0)
        else:
            self.prog[eng].append(("i", fn, None, 0))
        self._record((eng, n), reads, writes)

    def dma(self, q, slot, fn, reads=(), writes=(), inc=16):
        self.slot(slot)
        deps = self._deps(q, reads, writes)
        if self.cnt[slot] > 0:
            deps[slot] = max(deps.get(slot, 0), self.cnt[slot])
        self._emit_waits(q, deps)
        n = self.cnt[slot] + inc
        self.cnt[slot] = n
        self.prog[q].append(("i", fn, self.semh[slot], inc))
        self.snap[(slot, n)] = dict(self.clock[q])
        self._record((slot, n), reads, writes)

    def final_wait(self, eng, slots):
        for s in slots:
            if self.cnt.get(s, 0) > 0:
                self.prog[eng].append(("w", self.semh[s], self.cnt[s]))

    def emit(self):
        nc = self.nc
        prog = self.prog

        def run(e, lst):
            for it in lst:
                if it[0] == "w":
                    e.wait_ge(it[1], it[2])
                else:
                    ins = it[1](e)
                    if it[2] is not None:
                        ins.then_inc(it[2], it[3])

        with nc.Block() as block:
            @block.tensor
            def _(e):
                run(e, prog["pe"])

            @block.scalar
            def _(e):
                run(e, prog["act"])

            @block.vector
            def _(e):
                run(e, prog["dve"])

            @block.gpsimd
            def _(e):
                run(e, prog["pool"])

            @block.sync
            def _(e):
                run(e, prog["sp"])


def blk_res(name, lo, hi, blk=1024):
    return [(name, b) for b in range(lo // blk, (hi - 1) // blk + 1)]


def build_program(cfg):
    c = cfg
    D, KC, T, NS, NB, NT, TOK, FC, G, NQ, NKV, AWC = c.D, c.KC, c.T, c.NS, c.NB, c.NT, c.TOK, c.FC, c.G, c.NQ, c.NKV, c.AWC
    DEPTH = c.DEPTH
    nc = bass.Bass("TRN2", target_bir_lowering=False)
    es = ExitStack()

    def din(name, shape, dt=F32):
        return nc.dram_tensor(name, list(shape), dt, kind="ExternalInput").ap()

    xin = din("xin", [128, KC, TOK])
    cact_in = din("cvec", [128, KC])
    vecs_in = din("vecs", [DEPTH, 128, c.NV])
    ada_w = din("ada_w", [DEPTH, 6 * KC, 128, KC * 128])
    w_in = din("w_in", [DEPTH, c.MC, 128, KC * 128])
    lru_wa = din("lru_wa", [DEPTH, 128, KC * 128])
    lru_wx = din("lru_wx", [DEPTH, 128, KC * 128])
    w_lo = din("w_lo", [DEPTH, KC, 128, KC * 128])
    w_ao = din("w_ao", [DEPTH, KC, 128, AWC * 128])
    w_o = din("w_o", [DEPTH, KC, 128, KC * 128])
    w_f1 = din("w_f1", [DEPTH, 2 * FC, 128, KC * 128])
    w_f2 = din("w_f2", [DEPTH, KC, 128, FC * 128])
    yout = nc.dram_tensor("y", [128, KC, TOK], F32, kind="ExternalOutput").ap()
    xs = nc.dram_tensor("xs", [128, KC, TOK], F32, kind="Internal").ap()
    flag_in = din("flag", [128, 1])
    snd = [nc.dram_tensor("snd%d" % l, [128, 5 * KC + 2 * NKV * 128], F32).ap() for l in range(DEPTH)]
    rcv = [nc.dram_tensor("rcv%d" % l, [256, 5 * KC + 2 * NKV * 128], F32).ap() for l in range(DEPTH)]

    def sb(name, shape, dt):
        return es.enter_context(nc.sbuf_tensor(name, list(shape), dt))

    S = Sch(nc, es)

    ASZ = (3 * KC + AWC) * T
    arena = sb("arena", [128, ASZ], BF16)
    o_H, o_YL, o_MG, o_YA = 0, KC * T, 2 * KC * T, 3 * KC * T

    def aview(off, nchunk, dt=BF16):
        n = nchunk * T * (2 if dt == F32 else 1)
        v = arena[:, off:off + n]
        if dt == F32:
            v = v.bitcast(F32)
        return v.rearrange("p (c t) -> p c t", t=T)

    def ares(off, ci, t0, t1, dt=BF16):
        esz = 4 if dt == F32 else 2
        lo = off * 2 + (ci * T + t0) * esz
        hi = off * 2 + (ci * T + t1) * esz
        return blk_res("A", lo, hi)

    H = aview(o_H, KC)
    YL = aview(o_YL, KC)
    MG = aview(o_MG, KC)
    YA = aview(o_YA, AWC)
    XT = aview(o_YL, KC, F32)
    X2 = aview(o_H, KC, F32)
    H2 = aview(o_YA, KC)
    HID = aview(o_H, FC)
    assert FC <= 3 * KC and AWC >= KC

    NU = 7 * 1024 + 512
    LRU_SZ = 2 * NU + 2 * (T + 8)
    ATT_SZ = G * T + (128 + T) + T + (NB + 1) * 128 + 4 * 512 * 2 + 2 * 1024 * 2
    SSZ = max(LRU_SZ, ATT_SZ)
    scr = sb("scr", [128, SSZ], BF16)

    def sres(off, n):
        return blk_res("S", off * 2, (off + n) * 2, blk=256)

    def sview(off, n, dt=BF16):
        m = n * (2 if dt == F32 else 1)
        v = scr[:, off:off + m]
        return (v.bitcast(F32) if dt == F32 else v), sres(off, m)

    lru_sets = []
    p = 0
    LNAMES = ("G", "UC", "R1", "A2", "I1", "HL", "GL")
    for u in range(2):
        st = {}
        for nm in LNAMES:
            st[nm] = sview(p, 512, F32)
            p += 1024
        st["UCB"] = sview(p, 512, BF16)
        p += 512
        lru_sets.append(st)
    UB = []
    for u in range(2):
        UB.append(sview(p, T + 8, BF16))
        p += (T + 8)
    if KC * T >= 2 * NU:
        xplaces = [(arena, "A", o_MG, 1024), (arena, "A", o_YA, 1024)]
    else:
        xten = sb("xtra", [128, 4 * NU], BF16)
        xplaces = [(xten, "X", 0, 256), (xten, "X", 2 * NU, 256)]
    for (xten_, xname, xbase, xblk) in xplaces:
        q = xbase
        for u in range(2):
            st = {}
            for nm in LNAMES + ("UCB",):
                n_ = 512 if nm == "UCB" else 1024
                v = xten_[:, q:q + n_]
                st[nm] = ((v if nm == "UCB" else v.bitcast(F32)), blk_res(xname, q * 2, (q + n_) * 2, blk=xblk))
                q += n_
            lru_sets.append(st)
    p = 0
    oQG = p
    QG_ap, QG_res = sview(p, G * T); p += G * T
    QG = QG_ap.rearrange("p (g t) -> p g t", t=T)
    oKT = p
    KT, KT_res = sview(p, 128 + T); p += 128 + T
    oVT = p
    VT, VT_res = sview(p, T); p += T
    oVTOK = p
    VTOK_ap, VTOK_res = sview(p, (NB + 1) * 128); p += (NB + 1) * 128
    qgr = lambda j, t0, t1: sres(oQG + j * T + t0, t1 - t0)
    ktr = lambda a, b: sres(oKT + a, b - a)
    vtr = lambda a, b: sres(oVT + a, b - a)
    vkr = lambda b0, b1: sres(oVTOK + b0 * 128, (b1 - b0) * 128)
    VTOK = VTOK_ap.rearrange("p (b d) -> p b d", d=128)
    att_sets = []
    for u in range(2):
        st = {}
        for nm in ("ERP", "ERC", "EP", "EC"):
            st[nm] = sview(p, 512); p += 512
        att_sets.append(st)
    for u in range(2):
        att_sets[u]["DEN"] = sview(p, 512, F32); p += 1024
    assert p <= SSZ

    vecs = [sb("vecs%d" % l, [128, c.NV], F32) for l in range(DEPTH)]
    dv = [sb("dv%d" % l, [128, c.NDV], F32) for l in range(DEPTH)]
    cact32 = sb("cact32", [128, KC], F32)
    cact = sb("cact", [128, KC], BF16)
    ones_bf = sb("ones_bf", [128, 128], BF16)
    ident_bf = sb("ident_bf", [128, 128], BF16)
    _a, identf_r = sview(0, 128, F32)
    ident_f = _a
    _a, mcurf_r = sview(1024, G * 128, F32)
    mcur_f = _a.rearrange("p (g q) -> p g q", g=G)
    _a, mprevf_r = sview(1024 + 2 * G * 128, G * 128, F32)
    mprev_f = _a.rearrange("p (g q) -> p g q", g=G)
    mcur = sb("mcur", [128, G * 128], BF16)
    mprev0 = sb("mprev0", [128, G * 128], BF16)
    flag = sb("flag_sb", [128, 1], F32)
    WX = 5 * KC + 2 * NKV * 128
    mprev = sb("mprev", [128, G * 128], BF16)
    lstate = sb("lstate", [128, KC], F32)
    carry_u = sb("carry_u", [128, KC, 4], BF16)
    DG = [sb("dg%d" % i, [128, 4, 128], BF16) for i in range(2)]
    carry_k = sb("carry_k", [128, NKV, 128], BF16)
    carry_v = sb("carry_v", [128, NKV, 128], BF16)
    SQ = [lru_sets[i]["UCB"] for i in range(2)]
    TMPN = [lru_sets[i]["UC"] for i in range(2)]
    RS, RSr = lru_sets[0]["R1"]
    RI, RIr = lru_sets[0]["A2"]
    XCB = [sb("xcb%d" % i, [128, 512], F32) for i in range(2)]
    NRING = 6
    ring = [sb("ring%d" % i, [128, 16 * 128], BF16) for i in range(NRING)]
    ps = es.enter_context(nc.psum_tensor("ps", [128, 8, 512], F32))

    st_ring = {"i": 0}
    st_ps = {"i": 0}

    st_ps["n"] = 8

    def pbank(fixed=None):
        if fixed is not None:
            return ps[:, fixed, :], [("ps", fixed)]
        i = st_ps["i"] % st_ps["n"]
        st_ps["i"] += 1
        return ps[:, i, :], [("ps", i)]

    def wload(src2d, kcn):
        i = st_ring["i"] % NRING
        st_ring["i"] += 1
        dst = ring[i][:, 0:kcn * 128]
        res = [("ring", i)]
        S.dma("pool", "ring%d" % i, lambda e, dst=dst, src=src2d: e.dma_start(out=dst, in_=src), reads=(), writes=res)
        return dst.rearrange("p (k m) -> p k m", m=128), res

    def mm_acc(pb, pres, pairs, extra_reads=()):
        n = len(pairs)
        for i, (l, r, rd) in enumerate(pairs):
            S.op("pe", lambda e, l=l, r=r, i=i: e.matmul(pb, lhsT=l, rhs=r, start=(i == 0), stop=(i == n - 1)),
                 reads=list(rd) + list(extra_reads), writes=pres, inc=(i == n - 1))

    def act(out, in_, func, reads, writes, bias=None, scale=None):
        kw = {}
        if bias is not None:
            kw["bias"] = bias
        if scale is not None:
            kw["scale"] = scale
        S.op("act", lambda e: e.activation(out=out, in_=in_, func=func, **kw), reads=reads, writes=writes)

    VR = lambda l: [("vecs", l)]
    DR = lambda l, nm: [("dv", l, nm)]
    S.op("dve", lambda e: e.memset(ones_bf[:], 1.0), writes=[("ones",)])
    S.op("pool", lambda e: e.memset(ident_f, 1.0), writes=identf_r)
    S.op("pool", lambda e: e.affine_select(out=ident_f, in_=ident_f, compare_op=ALU.is_equal, fill=0.0, base=0,
                                           pattern=[[-1, 128]], channel_multiplier=1),
         reads=identf_r, writes=identf_r)
    S.op("pool", lambda e: e.memset(mcur_f, 1.0), writes=mcurf_r)
    S.op("pool", lambda e: e.affine_select(out=mcur_f, in_=mcur_f, compare_op=ALU.is_ge, fill=0.0, base=0,
                                           pattern=[[0, G], [1, 128]], channel_multiplier=-1),
         reads=mcurf_r, writes=mcurf_r)
    S.op("pool", lambda e: e.memset(mprev_f, 1.0), writes=mprevf_r)
    S.op("pool", lambda e: e.affine_select(out=mprev_f, in_=mprev_f, compare_op=ALU.is_gt, fill=0.0, base=0,
                                           pattern=[[0, G], [-1, 128]], channel_multiplier=1),
         reads=mprevf_r, writes=mprevf_r)
    S.op("dve", lambda e: e.tensor_copy(out=ident_bf[:], in_=ident_f), reads=identf_r, writes=[("ident",)])
    S.op("dve", lambda e: e.tensor_copy(out=mcur[:].rearrange("p (g q) -> p g q", g=G), in_=mcur_f), reads=mcurf_r, writes=[("mcur",)])
    S.op("dve", lambda e: e.tensor_copy(out=mprev[:].rearrange("p (g q) -> p g q", g=G), in_=mprev_f), reads=mprevf_r, writes=[("mprev",)])
    S.dma("sp", "ld_c", lambda e: e.dma_start(out=cact32[:], in_=cact_in[:, :]), writes=[("cact32",)])
    S.dma("sp", "ld_f", lambda e: e.dma_start(out=flag[:], in_=flag_in[:, :]), writes=[("flag",)])
    S.op("dve", lambda e: e.tensor_scalar(out=mprev0[:], in0=mprev[:], scalar1=flag[:, 0:1], scalar2=None, op0=ALU.mult),
         reads=[("mprev",), ("flag",)], writes=[("mprev0",)])
    for l in range(DEPTH):
        S.dma("sp", "ld_v%d" % l, lambda e, l=l: e.dma_start(out=vecs[l][:], in_=vecs_in[l, :, :]), writes=VR(l))
    act(cact[:], cact32[:], AF.Silu, [("cact32",)], [("cact",)])

    def vcol(l, nm, i, n=1):
        o = c.VO[nm] + i
        return vecs[l][:, o:o + n]

    def dcol(l, nm, i, n=1):
        o = c.DO[nm] + i
        return dv[l][:, o:o + n]

    ada_q = [(l_, j_) for l_ in range(DEPTH) for j_ in range(6 * KC)]
    ada_state = {"derived": set()}

    def DRm(l, part):
        return [("dv", l, "mod", part)]

    def MODR(l):
        return [("dv", l, "mod", p_) for p_ in range(6)]

    def ada_job(l, j):
        w, wres = wload(ada_w[l, j, :, :], KC)
        pb, pres = pbank()
        for kc in range(KC):
            S.op("pe", lambda e, kc=kc, w=w, pb=pb: e.matmul(pb[:, 0:1], lhsT=w[:, kc, :], rhs=cact[:, kc:kc + 1],
                                                         start=(kc == 0), stop=(kc == KC - 1)),
                 reads=wres + [("cact",)], writes=pres, inc=(kc == KC - 1))
        S.op("dve", lambda e, pb=pb: e.tensor_tensor(out=dcol(l, "mod", j), in0=pb[:, 0:1], in1=vcol(l, "ada_b", j), op=ALU.add),
             reads=pres + VR(l), writes=DRm(l, j // KC))

    def ada_tick(n=1):
        for _ in range(n):
            if ada_q:
                ada_job(*ada_q.pop(0))

    def ada_need(l, part):
        while ada_q and ada_q[0] <= (l, (part + 1) * KC - 1):
            ada_job(*ada_q.pop(0))
        if part >= 1 and (l, "s1") not in ada_state["derived"]:
            ada_state["derived"].add((l, "s1"))
            S.op("dve", lambda e: e.scalar_tensor_tensor(out=dcol(l, "s1", 0, KC), in0=dcol(l, "mod", KC, KC), scalar=1.0,
                                                         in1=vcol(l, "n1g", 0, KC), op0=ALU.add, op1=ALU.mult),
                 reads=DRm(l, 1) + VR(l), writes=DR(l, "s1"))
        if part >= 4 and (l, "s2") not in ada_state["derived"]:
            ada_state["derived"].add((l, "s2"))
            S.op("dve", lambda e: e.scalar_tensor_tensor(out=dcol(l, "s2", 0, KC), in0=dcol(l, "mod", 4 * KC, KC), scalar=1.0,
                                                         in1=vcol(l, "n2g", 0, KC), op0=ALU.add, op1=ALU.mult),
                 reads=DRm(l, 4) + VR(l), writes=DR(l, "s2"))

    def layer_setup(l):
        zc, tc_ = dcol(l, "z", 0, KC), dcol(l, "t", 0, KC)
        act(zc, vcol(l, "lam", 0, KC), AF.Exp, VR(l), DR(l, "z"), scale=-1.0)
        S.op("dve", lambda e: e.tensor_scalar(out=tc_, in0=zc, scalar1=0.2, scalar2=-0.25, op0=ALU.mult, op1=ALU.add),
             reads=DR(l, "z"), writes=DR(l, "t"))
        for cst in (1.0 / 3.0, -0.5, 1.0):
            S.op("dve", lambda e: e.tensor_tensor(out=tc_, in0=tc_, in1=zc, op=ALU.mult), reads=DR(l, "z") + DR(l, "t"), writes=DR(l, "t"))
            S.op("dve", lambda e, cst=cst: e.tensor_scalar(out=tc_, in0=tc_, scalar1=cst, scalar2=None, op0=ALU.add),
                 reads=DR(l, "t"), writes=DR(l, "t"))
        S.op("dve", lambda e: e.scalar_tensor_tensor(out=dcol(l, "nc8", 0, KC), in0=tc_, scalar=-LRU_C, in1=zc, op0=ALU.mult, op1=ALU.mult),
             reads=DR(l, "z") + DR(l, "t"), writes=DR(l, "nc8"))
        S.op("dve", lambda e: e.tensor_scalar(out=dcol(l, "nc16", 0, KC), in0=dcol(l, "nc8", 0, KC), scalar1=2.0, scalar2=None, op0=ALU.mult),
             reads=DR(l, "nc8"), writes=DR(l, "nc16"))
        act(dcol(l, "esk", 0, NQ), vcol(l, "sink", 0, NQ), AF.Exp, VR(l), DR(l, "esk"))
        S.op("dve", lambda e: e.tensor_scalar(out=dcol(l, "hnc8", 0, KC), in0=dcol(l, "nc8", 0, KC), scalar1=0.5, scalar2=None, op0=ALU.mult),
             reads=DR(l, "nc8"), writes=DR(l, "hnc8"))
        S.op("dve", lambda e: e.tensor_scalar(out=dcol(l, "hba", 0, KC), in0=vcol(l, "ba", 0, KC), scalar1=0.5, scalar2=None, op0=ALU.mult),
             reads=VR(l), writes=DR(l, "hba"))
        S.op("dve", lambda e: e.tensor_scalar(out=dcol(l, "hbx", 0, KC), in0=vcol(l, "bx", 0, KC), scalar1=0.5, scalar2=None, op0=ALU.mult),
             reads=VR(l), writes=DR(l, "hbx"))

    def layer_state_reset(l):
        S.op("dve", lambda e: e.memset(lstate[:], 0.0), writes=[("lstate", i) for i in range(KC)])
        S.op("dve", lambda e: e.memset(carry_u[:], 0.0), writes=[("carry_u", i) for i in range(KC)])
        S.op("dve", lambda e: e.memset(carry_k[:], 0.0), writes=[("carry_k", i) for i in range(NKV)])
        S.op("dve", lambda e: e.memset(carry_v[:], 0.0), writes=[("carry_v", i) for i in range(NKV)])

    def norm_stats(XV, x_off, t0, t1):
        pb, pres = pbank()
        for kc in range(KC):
            sq, sqr = SQ[kc % 2]
            xr = ares(x_off, kc, t0, t1, F32)
            act(sq, XV[:, kc, t0:t1], AF.Square, xr, sqr)
            S.op("pe", lambda e, sq=sq, kc=kc: e.matmul(pb, lhsT=ones_bf[:], rhs=sq, start=(kc == 0), stop=(kc == KC - 1)),
                 reads=sqr + [("ones",)], writes=pres, inc=True)
        act(RS, pb, AF.Sqrt, pres + [("eps",)], RSr, bias=EPS_AP[:, 0:1], scale=1.0 / D)
        S.op("dve", lambda e: e.reciprocal(out=RI, in_=RS), reads=RSr, writes=RIr)

    def norm_mod(l, XV, x_off, OUT, out_off, scol, bcol, mpart):
        for ns in range(NS):
            t0, t1 = ns * 512, (ns + 1) * 512
            norm_stats(XV, x_off, t0, t1)
            for kc in range(KC):
                tm, tmr = TMPN[kc % 2]
                xr = ares(x_off, kc, t0, t1, F32)
                S.op("dve", lambda e, tm=tm, kc=kc, t0=t0, t1=t1: e.scalar_tensor_tensor(out=tm, in0=XV[:, kc, t0:t1], scalar=scol(kc), in1=RI,
                                                                                     op0=ALU.mult, op1=ALU.mult),
                     reads=xr + RIr + DR(l, "s1") + DR(l, "s2"), writes=tmr)
                act(OUT[:, kc, t0:t1], tm, AF.Identity, tmr + DRm(l, mpart), ares(out_off, kc, t0, t1), bias=bcol(kc))

    EPS_AP = sb("eps_ap", [128, 1], F32)
    S.op("dve", lambda e: e.memset(EPS_AP[:], EPS), writes=[("eps",)])

    def proj(w, wres, RHS, rhs_off, nk, ns):
        t0, t1 = ns * 512, (ns + 1) * 512
        pb, pres = pbank()
        mm_acc(pb, pres, [(w[:, k, :], RHS[:, k, t0:t1], wres + ares(rhs_off, k, t0, t1)) for k in range(nk)])
        return pb, pres

    def load_norm1(l, tt):
        tok0 = tt * T
        xsrc = xin if l == 0 else xs
        xsrc_name = "xin" if l == 0 else "xs"
        for kc in range(KC):
            S.dma("sp", "ld_xt%d" % (kc % 4), lambda e, kc=kc: e.dma_start(out=XT[:, kc, :], in_=xsrc[:, kc, tok0:tok0 + T]),
                  reads=[(xsrc_name, kc, tt, n_) for n_ in range(NS)], writes=ares(o_YL, kc, 0, T, F32))
        ada_need(l, 1)
        norm_mod(l, XT, o_YL, H, o_H, lambda kc: dcol(l, "s1", kc), lambda kc: dcol(l, "mod", 0 * KC + kc), 0)

    def lru_branch(l, state_only):
        st_ps["n"] = 4
        st_ps["i"] = 0
        GC0 = math.sqrt(0.044715)
        GC1 = 0.7978845608028654
        pend = {"q": None, "p2": None, "dve1": None, "dve2": None}

        def pre(ch):
            cx = {}
            ub, _ = UB[ch % 2]
            cx["ub"], cx["ubname"], cx["dg"] = ub, "ubh%d" % (ch % 2), DG[ch % 2]
            dg = cx["dg"]
            S.op("dve", lambda e, ub=ub, ch=ch: e.tensor_copy(out=ub[:, 1:4], in_=carry_u[:, ch, 0:3]),
                 reads=[("carry_u", ch)], writes=[(cx["ubname"],)])
            for k in range(4):
                S.op("dve", lambda e, dg=dg, k=k, ch=ch: e.tensor_scalar(out=dg[:, k, :], in0=ident_bf[:], scalar1=vcol(l, "cw", k * KC + ch),
                                                                       scalar2=None, op0=ALU.mult),
                     reads=[("ident",)] + VR(l), writes=[("dg", ch % 2, k)])
            cx["ubr"] = [[("ub", ch % 2, ns)] for ns in range(NS)]
            return cx

        def pre_b(ch, cx):
            ada_tick(2)
            if not state_only:
                cx["wg"] = wload(w_in[l, c.GT0 + ch, :, :], KC)
            cx["wa"] = wload(lru_wa[l, :, ch * 128:(ch + 1) * 128], 1)
            cx["wx"] = wload(lru_wx[l, :, ch * 128:(ch + 1) * 128], 1)

        def uproj(ch, ns, cx):
            ub = cx["ub"]
            if "wu" not in cx:
                cx["wu"] = wload(w_in[l, c.U0 + ch, :, :], KC)
            wu, wures = cx["wu"]
            pb, pres = proj(wu, wures, H, o_H, KC, ns)
            S.op("dve", lambda e, ub=ub, pb=pb, ns=ns, ch=ch: e.tensor_scalar(out=ub[:, 4 + ns * 512:4 + (ns + 1) * 512], in0=pb,
                                                                             scalar1=vcol(l, "b_in", c.U0 + ch), scalar2=None, op0=ALU.add),
                 reads=pres + VR(l), writes=cx["ubr"][ns])
            if ns == NS - 1:
                S.op("dve", lambda e, ub=ub, ch=ch: e.tensor_copy(out=carry_u[:, ch, 0:3], in_=ub[:, T + 1:T + 4]),
                     reads=cx["ubr"][NS - 1], writes=[("carry_u", ch)])

        def gproj(ch, ns, cx, st):
            Gv, Gr = st["G"]
            wg, wgres = cx["wg"]
            pb, pres = proj(wg, wgres, H, o_H, KC, ns)
            S.op("dve", lambda e, Gv=Gv, pb=pb, ch=ch: e.tensor_scalar(out=Gv, in0=pb, scalar1=vcol(l, "b_in", c.GT0 + ch), scalar2=None, op0=ALU.add),
                 reads=pres + VR(l), writes=Gr)

        def conv(ch, ns, cx, st):
            ub, dg, ubr = cx["ub"], cx["dg"], cx["ubr"]
            UCB, UCBr = st["UCB"]
            o = 4 + ns * 512
            ubrd = ubr[ns] + ([(cx["ubname"],)] if ns == 0 else ubr[ns - 1])
            pcv, pcvres = pbank()
            for k in range(4):
                j = 3 - k
                S.op("pe", lambda e, pcv=pcv, dg=dg, ub=ub, k=k, j=j, o=o: e.matmul(pcv, lhsT=dg[:, k, :], rhs=ub[:, o - j:o - j + 512],
                                                                               start=(k == 0), stop=(k == 3)),
                     reads=[("dg", ch % 2, k)] + ubrd, writes=pcvres, inc=(k == 3))
            S.op("dve", lambda e, UCB=UCB, pcv=pcv, ch=ch: e.tensor_scalar(out=UCB, in0=pcv, scalar1=vcol(l, "cb", ch), scalar2=None, op0=ALU.add),
                 reads=pcvres + VR(l), writes=UCBr)

        def gates(ch, ns, cx, st):
            UCB, UCBr = st["UCB"]
            wa_c, wares = cx["wa"]
            wx_c, wxres = cx["wx"]
            pa, pares = pbank(fixed=4 + 2 * (ns % 2))
            S.op("pe", lambda e, pa=pa, wa_c=wa_c, UCB=UCB: e.matmul(pa, lhsT=wa_c[:, 0, :], rhs=UCB, start=True, stop=True),
                 reads=wares + UCBr, writes=pares)
            px, pxres = pbank(fixed=5 + 2 * (ns % 2))
            S.op("pe", lambda e, px=px, wx_c=wx_c, UCB=UCB: e.matmul(px, lhsT=wx_c[:, 0, :], rhs=UCB, start=True, stop=True),
                 reads=wxres + UCBr, writes=pxres)
            return (pa, pares, px, pxres)

        assert NS == 2
        cx = pre(0)
        uproj(0, 0, cx)
        uproj(0, 1, cx)
        for ch in range(KC):
            pre_b(ch, cx)
            nxt = pre(ch + 1) if ch + 1 < KC else None
            sets = [lru_sets[2 * (ch % 3) + ns] for ns in range(NS)]
            banks = [None, None]
            if not state_only:
                gproj(ch, 0, cx, sets[0])
            conv(ch, 0, cx, sets[0])
            if pend["q"] is not None:
                pend["q"]()
            if pend["dve2"] is not None:
                pend["dve2"](0)
            if not state_only:
                gproj(ch, 1, cx, sets[1])
            elif nxt is not None:
                uproj(ch + 1, 0, nxt)
            banks[0] = gates(ch, 0, cx, sets[0])
            if pend["p2"] is not None:
                pend["p2"]()
            conv(ch, 1, cx, sets[1])
            if pend["dve2"] is not None:
                pend["dve2"](1)
            if nxt is not None:
                uproj(ch + 1, 0 if not state_only else 1, nxt)
            banks[1] = gates(ch, 1, cx, sets[1])
            if nxt is not None and not state_only:
                uproj(ch + 1, 1, nxt)
            cx = nxt
            for ns in range(NS):
                st = sets[ns]
                pa, pares, px, pxres = banks[ns]
                R1, R1r = st["R1"]; I1, I1r = st["I1"]
                act(R1, pa, AF.Tanh, pares + DR(l, "hba"), R1r, bias=dcol(l, "hba", ch), scale=0.5)
                act(I1, px, AF.Tanh, pxres + DR(l, "hbx"), I1r, bias=dcol(l, "hbx", ch), scale=0.5)
            for ns in range(NS):
                st = sets[ns]
                R1, R1r = st["R1"]; A2, A2r = st["A2"]
                act(A2, R1, AF.Exp, R1r + DR(l, "nc8"), A2r, bias=dcol(l, "nc8", ch), scale=dcol(l, "nc8", ch))
                act(R1, R1, AF.Exp, R1r + DR(l, "hnc8"), R1r, bias=dcol(l, "hnc8", ch), scale=dcol(l, "hnc8", ch))
            if not state_only:
                for ns in range(NS):
                    st = sets[ns]
                    Gv, Gr = st["G"]; GL, GLr = st["GL"]
                    act(GL, Gv, AF.Square, Gr, GLr, scale=GC0)

            def q_dve(sets=sets):
                if state_only:
                    return
                for ns in range(NS):
                    Gv, Gr = sets[ns]["G"]; GL, GLr = sets[ns]["GL"]
                    S.op("dve", lambda e, GL=GL, Gv=Gv: e.scalar_tensor_tensor(out=GL, in0=GL, scalar=1.0, in1=Gv, op0=ALU.add, op1=ALU.mult),
                         reads=GLr + Gr, writes=GLr)

            def part2(sets=sets):
                if not state_only:
                    for ns in range(NS):
                        GL, GLr = sets[ns]["GL"]
                        act(GL, GL, AF.Tanh, GLr, GLr, scale=GC1)
                for ns in range(NS):
                    A2, A2r = sets[ns]["A2"]
                    act(A2, A2, AF.Sqrt, A2r + [("one",)], A2r, bias=ONE_AP[:, 0:1], scale=-1.0)

            def back_dve(ns, ch=ch, sets=sets):
                st = sets[ns]
                Gv, Gr = st["G"]; R1, R1r = st["R1"]; A2, A2r = st["A2"]; UCB, UCBr = st["UCB"]
                I1, I1r = st["I1"]; HL, HLr = st["HL"]; GL, GLr = st["GL"]
                S.op("dve", lambda e: e.scalar_tensor_tensor(out=I1, in0=I1, scalar=1.0, in1=UCB, op0=ALU.add, op1=ALU.mult),
                     reads=I1r + UCBr, writes=I1r)
                S.op("dve", lambda e: e.scalar_tensor_tensor(out=I1, in0=I1, scalar=0.5, in1=A2, op0=ALU.mult, op1=ALU.mult),
                     reads=I1r + A2r, writes=I1r)
                if ns == 0:
                    init, initr = lstate[:, ch:ch + 1], [("lstate", ch)]
                else:
                    init, initr = sets[ns - 1]["HL"][0][:, 511:512], sets[ns - 1]["HL"][1]
                S.op("dve", lambda e: e.tensor_tensor_scan(out=HL, data0=R1, data1=I1, initial=init, op0=ALU.mult, op1=ALU.add),
                     reads=R1r + I1r + initr, writes=HLr)
                if ns == NS - 1:
                    S.op("dve", lambda e: e.tensor_copy(out=lstate[:, ch:ch + 1], in_=HL[:, 511:512]), reads=HLr, writes=[("lstate", ch)])
                if not state_only:
                    S.op("dve", lambda e: e.scalar_tensor_tensor(out=GL, in0=GL, scalar=1.0, in1=Gv, op0=ALU.add, op1=ALU.mult),
                         reads=GLr + Gr, writes=GLr)
                    S.op("dve", lambda e: e.scalar_tensor_tensor(out=YL[:, ch, ns * 512:(ns + 1) * 512], in0=HL, scalar=0.5, in1=GL,
                                                                 op0=ALU.mult, op1=ALU.mult),
                         reads=HLr + GLr, writes=ares(o_YL, ch, ns * 512, (ns + 1) * 512))

            pend["dve2"] = pend["dve1"]
            pend["dve1"] = back_dve
            pend["q"] = q_dve
            pend["p2"] = part2
        if pend["q"] is not None:
            pend["q"]()
        if pend["p2"] is not None:
            pend["p2"]()
        for key in ("dve2", "dve1"):
            if pend[key] is not None:
                pend[key](0)
                pend[key](1)
        st_ps["n"] = 8

    def kv_halo(l):
        VTl, VTlr = lru_sets[0]["UCB"]
        for g in range(NKV):
            w, wres = wload(w_in[l, c.K0 + g, :, :], KC)
            pb, pres = pbank()
            mm_acc(pb[:, 0:128], pres, [(w[:, k, :], H[:, k, T - 128:T], wres + ares(o_H, k, T - 128, T)) for k in range(KC)])
            act(carry_k[:, g, :], pb[:, 0:128], AF.Identity, pres + VR(l), [("carry_k", g)], bias=vcol(l, "b_in", c.K0 + g))
            w, wres = wload(w_in[l, c.V0 + g, :, :], KC)
            pb, pres = pbank()
            mm_acc(pb[:, 0:128], pres, [(w[:, k, :], H[:, k, T - 128:T], wres + ares(o_H, k, T - 128, T)) for k in range(KC)])
            act(VTl[:, 0:128], pb[:, 0:128], AF.Identity, pres + VR(l), VTlr, bias=vcol(l, "b_in", c.V0 + g))
            pb2, pres2 = pbank()
            pbb = pb2.bitcast(BF16)
            S.op("pe", lambda e, pbb=pbb, VTl=VTl: e.transpose(out=pbb[:, 0:128], in_=VTl[:, 0:128], identity=ident_bf[:]),
                 reads=VTlr + [("ident",)], writes=pres2)
            S.op("dve", lambda e, pbb=pbb, g=g: e.tensor_copy(out=carry_v[:, g, :], in_=pbb[:, 0:128]), reads=pres2, writes=[("carry_v", g)])

    def prepass_tile(l, tt):
        load_norm1(l, tt)
        lru_branch(l, True)
        if tt == NT - 1:
            kv_halo(l)

    def exchange_send(l):
        XS, XSr = sview(0, WX, F32)
        XR, XRr = sview(2 * WX, WX, F32)
        o1, o2, o3 = KC, 5 * KC, 5 * KC + NKV * 128
        S.op("dve", lambda e: e.tensor_copy(out=XS[:, 0:o1], in_=lstate[:]), reads=[("lstate", i) for i in range(KC)], writes=XSr)
        S.op("dve", lambda e: e.tensor_copy(out=XS[:, o1:o2], in_=carry_u[:].rearrange("p k f -> p (k f)")),
             reads=[("carry_u", i) for i in range(KC)], writes=XSr)
        S.op("dve", lambda e: e.tensor_copy(out=XS[:, o2:o3], in_=carry_k[:].rearrange("p g d -> p (g d)")),
             reads=[("carry_k", i) for i in range(NKV)], writes=XSr)
        S.op("dve", lambda e: e.tensor_copy(out=XS[:, o3:WX], in_=carry_v[:].rearrange("p g d -> p (g d)")),
             reads=[("carry_v", i) for i in range(NKV)], writes=XSr)
        S.dma("sp", "st_snd", lambda e: e.dma_start(out=snd[l][:, :], in_=XS), reads=XSr, writes=[("snd", l)])
        groups = [[2 * i, 2 * i + 1] for i in range(c.NCORES // 2)]
        S.dma("pool", "cc", lambda e: e.collective_compute("AllGather", ALU.bypass, replica_groups=groups,
                                                           ins=[snd[l].opt()], outs=[rcv[l].opt()]),
              reads=[("snd", l)], writes=[("rcv", l)], inc=1)

    def exchange_recv(l):
        XR, XRr = sview(2 * WX, WX, F32)
        o1, o2, o3 = KC, 5 * KC, 5 * KC + NKV * 128
        S.dma("sp", "ld_rcv", lambda e: e.dma_start(out=XR, in_=rcv[l][0:128, :]), reads=[("rcv", l)], writes=XRr)
        fl = flag[:, 0:1]
        S.op("dve", lambda e: e.tensor_scalar(out=lstate[:], in0=XR[:, 0:o1], scalar1=fl, scalar2=None, op0=ALU.mult),
             reads=XRr + [("flag",)], writes=[("lstate", i) for i in range(KC)])
        S.op("dve", lambda e: e.tensor_scalar(out=carry_u[:].rearrange("p k f -> p (k f)"), in0=XR[:, o1:o2], scalar1=fl, scalar2=None, op0=ALU.mult),
             reads=XRr + [("flag",)], writes=[("carry_u", i) for i in range(KC)])
        S.op("dve", lambda e: e.tensor_scalar(out=carry_k[:].rearrange("p g d -> p (g d)"), in0=XR[:, o2:o3], scalar1=fl, scalar2=None, op0=ALU.mult),
             reads=XRr + [("flag",)], writes=[("carry_k", i) for i in range(NKV)])
        S.op("dve", lambda e: e.tensor_scalar(out=carry_v[:].rearrange("p g d -> p (g d)"), in0=XR[:, o3:WX], scalar1=fl, scalar2=None, op0=ALU.mult),
             reads=XRr + [("flag",)], writes=[("carry_v", i) for i in range(NKV)])

    def layer_tile(l, tt, last_layer):
        tok0 = tt * T
        xsrc = xin if l == 0 else xs
        xsrc_name = "xin" if l == 0 else "xs"
        load_norm1(l, tt)
        if tt == 0 and c.SPLIT == 2:
            exchange_recv(l)
        lru_branch(l, False)

        sc = 1.0 / math.sqrt(128.0)
        aunit = 0
        for g in range(NKV):
            for j in range(G):
                m = c.Q0 + g * G + j
                w, wres = wload(w_in[l, m, :, :], KC)
                for ns in range(NS):
                    pb, pres = proj(w, wres, H, o_H, KC, ns)
                    act(QG[:, j, ns * 512:(ns + 1) * 512], pb, AF.Identity, pres + VR(l), qgr(j, ns * 512, (ns + 1) * 512), bias=vcol(l, "b_in", m))
            S.op("dve", lambda e, g=g: e.tensor_copy(out=KT[:, 0:128], in_=carry_k[:, g, :]), reads=[("carry_k", g)], writes=ktr(0, 128))
            w, wres = wload(w_in[l, c.K0 + g, :, :], KC)
            for ns in range(NS):
                pb, pres = proj(w, wres, H, o_H, KC, ns)
                act(KT[:, 128 + ns * 512:128 + (ns + 1) * 512], pb, AF.Identity, pres + VR(l), ktr(128 + ns * 512, 128 + (ns + 1) * 512), bias=vcol(l, "b_in", c.K0 + g))
            S.op("dve", lambda e, g=g: e.tensor_copy(out=carry_k[:, g, :], in_=KT[:, T:T + 128]), reads=ktr(T, T + 128), writes=[("carry_k", g)])
            w, wres = wload(w_in[l, c.V0 + g, :, :], KC)
            for ns in range(NS):
                pb, pres = proj(w, wres, H, o_H, KC, ns)
                act(VT[:, ns * 512:(ns + 1) * 512], pb, AF.Identity, pres + VR(l), vtr(ns * 512, (ns + 1) * 512), bias=vcol(l, "b_in", c.V0 + g))
            S.op("dve", lambda e, g=g: e.tensor_copy(out=VTOK[:, 0, :], in_=carry_v[:, g, :]), reads=[("carry_v", g)], writes=vkr(0, 1))
            for q4 in range(NB // 4):
                pb, pres = pbank()
                pbb = pb.bitcast(BF16)
                for i in range(4):
                    b = q4 * 4 + i
                    S.op("pe", lambda e, pbb=pbb, i=i, b=b: e.transpose(out=pbb[:, i * 128:(i + 1) * 128], in_=VT[:, b * 128:(b + 1) * 128], identity=ident_bf[:]),
                         reads=vtr(b * 128, (b + 1) * 128) + [("ident",)], writes=pres, inc=(i == 3))
                S.op("dve", lambda e, pbb=pbb, q4=q4: e.tensor_copy(out=VTOK[:, 1 + q4 * 4:1 + q4 * 4 + 4, :],
                                                                   in_=pbb[:, 0:512].rearrange("p (b d) -> p b d", d=128)),
                     reads=pres, writes=vkr(1 + q4 * 4, 5 + q4 * 4))
            S.op("dve", lambda e, g=g: e.tensor_copy(out=carry_v[:, g, :], in_=VTOK[:, NB, :]), reads=vkr(NB, NB + 1), writes=[("carry_v", g)])
            if last_layer and tt > 0:
                if g == 0:
                    final_norm_sub(tt - 1, 0)
                if g == NKV - 1:
                    final_norm_sub(tt - 1, 1)
            for b in range(NB):
                first = (tt == 0 and b == 0 and c.SPLIT == 1)
                mp = mprev0 if (tt == 0 and b == 0) else mprev
                st = att_sets[aunit % 2]
                aunit += 1
                ERP, ERPr = st["ERP"]; ERC, ERCr = st["ERC"]; EP, EPr = st["EP"]; EC, ECr = st["EC"]; DEN, DENr = st["DEN"]
                qrhs = QG[:, :, b * 128:(b + 1) * 128]
                qres = [r for j in range(G) for r in qgr(j, b * 128, (b + 1) * 128)]
                if not first:
                    pp, ppres = pbank()
                    S.op("pe", lambda e, pp=pp, b=b, qrhs=qrhs: e.matmul(pp.rearrange("p (g q) -> p g q", g=G), lhsT=KT[:, b * 128:(b + 1) * 128], rhs=qrhs, start=True, stop=True),
                         reads=qres + ktr(b * 128, (b + 1) * 128), writes=ppres)
                pc, pcres = pbank()
                S.op("pe", lambda e, pc=pc, b=b, qrhs=qrhs: e.matmul(pc.rearrange("p (g q) -> p g q", g=G), lhsT=KT[:, 128 + b * 128:128 + (b + 1) * 128], rhs=qrhs, start=True, stop=True),
                     reads=qres + ktr(128 + b * 128, 128 + (b + 1) * 128), writes=pcres)
                if not first:
                    act(ERP, pp, AF.Exp, ppres, ERPr, scale=sc)
                    S.op("dve", lambda e, EP=EP, ERP=ERP, mp=mp: e.tensor_tensor(out=EP, in0=ERP, in1=mp[:], op=ALU.mult), reads=ERPr + [("mprev",), ("mprev0",)], writes=EPr)
                act(ERC, pc, AF.Exp, pcres, ERCr, scale=sc)
                S.op("dve", lambda e, EC=EC, ERC=ERC: e.tensor_tensor(out=EC, in0=ERC, in1=mcur[:], op=ALU.mult), reads=ERCr + [("mcur",)], writes=ECr)
                pd, pdres = pbank()
                po, pores = pbank()
                if not first:
                    S.op("pe", lambda e, pd=pd, EP=EP: e.matmul(pd, lhsT=ones_bf[:], rhs=EP, start=True, stop=False), reads=EPr + [("ones",)], writes=pdres, inc=False)
                S.op("pe", lambda e, pd=pd, EC=EC, first=first: e.matmul(pd, lhsT=ones_bf[:], rhs=EC, start=first, stop=True), reads=ECr + [("ones",)], writes=pdres)
                if not first:
                    S.op("pe", lambda e, po=po, EP=EP, b=b: e.matmul(po, lhsT=VTOK[:, b, :], rhs=EP, start=True, stop=False), reads=EPr + vkr(b, b + 1), writes=pores, inc=False)
                S.op("pe", lambda e, po=po, EC=EC, b=b, first=first: e.matmul(po, lhsT=VTOK[:, b + 1, :], rhs=EC, start=first, stop=True), reads=ECr + vkr(b + 1, b + 2), writes=pores)
                for j in range(G):
                    S.op("dve", lambda e, DEN=DEN, pd=pd, j=j, g=g: e.tensor_scalar(out=DEN[:, j * 128:(j + 1) * 128], in0=pd[:, j * 128:(j + 1) * 128],
                                                                                 scalar1=dcol(l, "esk", g * G + j), scalar2=None, op0=ALU.add),
                         reads=pdres + DR(l, "esk"), writes=DENr)
                act(DEN, DEN, AF.Ln, DENr, DENr)
                act(DEN, DEN, AF.Exp, DENr, DENr, scale=-1.0)
                S.op("dve", lambda e, DEN=DEN, po=po, g=g, b=b: e.tensor_tensor(out=YA[:, g * G:(g + 1) * G, b * 128:(b + 1) * 128],
                                                                              in0=po.rearrange("p (g q) -> p g q", g=G),
                                                                              in1=DEN.rearrange("p (g q) -> p g q", g=G), op=ALU.mult),
                     reads=pores + DENr, writes=[r for j in range(G) for r in ares(o_YA, g * G + j, b * 128, (b + 1) * 128)])

        SA, SAr = lru_sets[0]["G"]; SB, SBr = lru_sets[0]["UC"]
        SA2, SA2r = lru_sets[1]["G"]; SB2, SB2r = lru_sets[1]["UC"]
        mu = 0
        for m in range(KC):
            ada_tick(1)
            wA, wAres = wload(w_in[l, c.GA0 + m, :, :], KC)
            wB, wBres = wload(w_in[l, c.GB0 + m, :, :], KC)
            wl, wlres = wload(w_lo[l, m, :, :], KC)
            wt, wtres = wload(w_ao[l, m, :, :], AWC)
            for ns in range(NS):
                sa, sar, sbb, sbr = (SA, SAr, SB, SBr) if mu % 2 == 0 else (SA2, SA2r, SB2, SB2r)
                mu += 1
                pA, pAres = proj(wA, wAres, H, o_H, KC, ns)
                pB, pBres = proj(wB, wBres, H, o_H, KC, ns)
                pL, pLres = proj(wl, wlres, YL, o_YL, KC, ns)
                pT, pTres = proj(wt, wtres, YA, o_YA, AWC, ns)
                act(sa, pA, AF.Sigmoid, pAres + VR(l), sar, bias=vcol(l, "b_in", c.GA0 + m))
                act(sbb, pB, AF.Sigmoid, pBres + VR(l), sbr, bias=vcol(l, "b_in", c.GB0 + m))
                S.op("dve", lambda e, sa=sa, pL=pL: e.tensor_tensor(out=sa, in0=sa, in1=pL, op=ALU.mult), reads=sar + pLres, writes=sar)
                S.op("dve", lambda e, sbb=sbb, pT=pT: e.tensor_tensor(out=sbb, in0=sbb, in1=pT, op=ALU.mult), reads=sbr + pTres, writes=sbr)
                S.op("dve", lambda e, sa=sa, sbb=sbb, m=m, ns=ns: e.tensor_tensor(out=MG[:, m, ns * 512:(ns + 1) * 512], in0=sa, in1=sbb, op=ALU.add),
                     reads=sar + sbr, writes=ares(o_MG, m, ns * 512, (ns + 1) * 512))

        ada_need(l, 2)
        for m in range(KC):
            wo, wores = wload(w_o[l, m, :, :], KC)
            for ns in range(NS):
                xi = (m * NS + ns) % 2
                xcb = XCB[xi]
                S.dma("sp", "ld_xc%d" % xi, lambda e, xcb=xcb, m=m, ns=ns: e.dma_start(out=xcb[:], in_=xsrc[:, m, tok0 + ns * 512:tok0 + (ns + 1) * 512]),
                      reads=[(xsrc_name, m, tt, ns)], writes=[("xcb", xi)])
                pO, pOres = proj(wo, wores, MG, o_MG, KC, ns)
                S.op("dve", lambda e, pO=pO, m=m, ns=ns, xcb=xcb: e.scalar_tensor_tensor(
                    out=X2[:, m, ns * 512:(ns + 1) * 512], in0=pO, scalar=dcol(l, "mod", 2 * KC + m), in1=xcb[:],
                    op0=ALU.mult, op1=ALU.add),
                     reads=pOres + DRm(l, 2) + [("xcb", xi)], writes=ares(o_H, m, ns * 512, (ns + 1) * 512, F32))
            S.dma("sp", "st_x%d" % (m % 4), lambda e, m=m: e.dma_start(out=xs[:, m, tok0:tok0 + T], in_=X2[:, m, :]),
                  reads=ares(o_H, m, 0, T, F32), writes=[("xs", m, tt, n_) for n_ in range(NS)])

        ada_need(l, 4)
        norm_mod(l, X2, o_H, H2, o_YA, lambda kc: dcol(l, "s2", kc), lambda kc: dcol(l, "mod", 3 * KC + kc), 3)
        SG, SGr = lru_sets[0]["G"]; SG2, SG2r = lru_sets[1]["G"]
        fu = 0
        for j in range(FC):
            ada_tick(1)
            wg, wgres = wload(w_f1[l, j, :, :], KC)
            wu, wures = wload(w_f1[l, FC + j, :, :], KC)
            for ns in range(NS):
                sg, sgr = (SG, SGr) if fu % 2 == 0 else (SG2, SG2r)
                fu += 1
                pg, pgres = proj(wg, wgres, H2, o_YA, KC, ns)
                pu, pures = proj(wu, wures, H2, o_YA, KC, ns)
                act(sg, pg, AF.Silu, pgres, sgr)
                S.op("dve", lambda e, sg=sg, pu=pu, j=j, ns=ns: e.tensor_tensor(out=HID[:, j, ns * 512:(ns + 1) * 512], in0=sg, in1=pu, op=ALU.mult),
                     reads=sgr + pures, writes=ares(o_H, j, ns * 512, (ns + 1) * 512))
        ada_need(l, 5)
        pieces = []
        k0 = 0
        while k0 < FC:
            pieces.append((k0, min(16, FC - k0)))
            k0 += 16
        XO = [lru_sets[0]["R1"], lru_sets[0]["A2"], lru_sets[1]["R1"], lru_sets[1]["A2"]]
        xo_i = 0
        for m in range(KC):
            ws = []
            for (k0, kn) in pieces:
                ws.append(wload(w_f2[l, m, :, k0 * 128:(k0 + kn) * 128], kn))
            for ns in range(NS):
                t0, t1 = ns * 512, (ns + 1) * 512
                xi = (m * NS + ns) % 2
                xcb = XCB[xi]
                S.dma("sp", "ld_xc%d" % xi, lambda e, xcb=xcb, m=m, t0=t0, t1=t1: e.dma_start(out=xcb[:], in_=xs[:, m, tok0 + t0:tok0 + t1]),
                      reads=[("xs", m, tt, ns)], writes=[("xcb", xi)])
                pb, pres = pbank()
                pairs = []
                for (k0, kn), (w, wres) in zip(pieces, ws):
                    for k in range(kn):
                        pairs.append((w[:, k, :], HID[:, k0 + k, t0:t1], wres + ares(o_H, k0 + k, t0, t1)))
                mm_acc(pb, pres, pairs)
                xo, xor_ = XO[xo_i % 4]
                xo_i += 1
                S.op("dve", lambda e, pb=pb, m=m, xo=xo, xcb=xcb: e.scalar_tensor_tensor(
                    out=xo, in0=pb, scalar=dcol(l, "mod", 5 * KC + m), in1=xcb[:], op0=ALU.mult, op1=ALU.add),
                     reads=pres + DRm(l, 5) + [("xcb", xi)], writes=xor_)
                S.dma("sp", "st_x%d" % (xo_i % 4), lambda e, xo=xo, m=m, t0=t0, t1=t1: e.dma_start(out=xs[:, m, tok0 + t0:tok0 + t1], in_=xo),
                      reads=xor_, writes=[("xs", m, tt, ns)])

        if last_layer and tt == NT - 1:
            for ns in range(NS):
                final_norm_sub(tt, ns)

    XF = arena[:, o_MG:o_MG + KC * T].bitcast(F32).rearrange("p (c t) -> p c t", t=512)

    def xfres(kc):
        return blk_res("A", o_MG * 2 + kc * 2048, o_MG * 2 + (kc + 1) * 2048)

    def final_norm_sub(tt, ns):
        tok = tt * T + ns * 512
        for kc in range(KC):
            S.dma("sp", "ld_xf%d" % (kc % 4), lambda e, kc=kc: e.dma_start(out=XF[:, kc, :], in_=xs[:, kc, tok:tok + 512]),
                  reads=[("xs", kc, tt, ns)], writes=xfres(kc))
        pb, pres = pbank()
        for kc in range(KC):
            i = kc % 2
            sq = DG[i][:].rearrange("p a b -> p (a b)")
            sqr = [("dg", i, k) for k in range(4)]
            act(sq, XF[:, kc, :], AF.Square, xfres(kc), sqr)
            S.op("pe", lambda e, sq=sq, kc=kc, pb=pb: e.matmul(pb, lhsT=ones_bf[:], rhs=sq, start=(kc == 0), stop=(kc == KC - 1)),
                 reads=sqr + [("ones",)], writes=pres, inc=True)
        RSx, RIx = XCB[0][:], XCB[1][:]
        act(RSx, pb, AF.Sqrt, pres + [("eps",)], [("xcb", 0)], bias=EPS_AP[:, 0:1], scale=1.0 / D)
        S.op("dve", lambda e: e.reciprocal(out=RIx, in_=RSx), reads=[("xcb", 0)], writes=[("xcb", 1)])
        for kc in range(KC):
            S.op("dve", lambda e, kc=kc: e.scalar_tensor_tensor(out=XF[:, kc, :], in0=XF[:, kc, :], scalar=vcol(0, "fg", kc), in1=RIx,
                                                                op0=ALU.mult, op1=ALU.mult),
                 reads=xfres(kc) + [("xcb", 1)] + VR(0), writes=xfres(kc))
            S.dma("sp", "st_y%d" % (kc % 4), lambda e, kc=kc: e.dma_start(out=yout[:, kc, tok:tok + 512], in_=XF[:, kc, :]),
                  reads=xfres(kc), writes=[("y", kc, tt, ns)])

    ONE_AP = sb("one_ap", [128, 1], F32)
    S.op("dve", lambda e: e.memset(ONE_AP[:], 1.0), writes=[("one",)])

    for l in range(DEPTH):
        layer_setup(l)
    for l in range(DEPTH):
        layer_state_reset(l)
        if c.SPLIT == 2:
            for tt in range(NT):
                prepass_tile(l, tt)
            exchange_send(l)
        for tt in range(NT):
            layer_tile(l, tt, l == DEPTH - 1)
    S.final_wait("sp", ["st_y%d" % i for i in range(4)] + ["st_x%d" % i for i in range(4)])
    S.emit()
    es.close()
    return nc


def _tile_w(W, kcn=None):
    K, N = W.shape
    kc, mc = K // 128, N // 128
    return np.ascontiguousarray(W.reshape(kc, 128, mc, 128).transpose(2, 1, 0, 3).reshape(mc, 128, kc * 128))


def _pcol(v):
    return np.ascontiguousarray(v.reshape(-1, 128).T)


def prep_shared(cfg, inp):
    c = cfg
    DEPTH, KC = c.DEPTH, c.KC
    f = lambda a: np.asarray(a, dtype=np.float32)
    sh = {}
    sh["ada_w"] = np.stack([_tile_w(f(inp["ada_w"][l])) for l in range(DEPTH)])
    sh["w_in"] = np.stack([_tile_w(f(inp["w_in"][l])) for l in range(DEPTH)])
    sh["lru_wa"] = np.stack([np.ascontiguousarray(f(inp["lru_wa"][l]).transpose(1, 0, 2).reshape(128, KC * 128)) for l in range(DEPTH)])
    sh["lru_wx"] = np.stack([np.ascontiguousarray(f(inp["lru_wx"][l]).transpose(1, 0, 2).reshape(128, KC * 128)) for l in range(DEPTH)])
    sh["w_lo"] = np.stack([_tile_w(f(inp["w_lru_out"][l])) for l in range(DEPTH)])
    sh["w_ao"] = np.stack([_tile_w(f(inp["w_attn_out"][l])) for l in range(DEPTH)])
    sh["w_o"] = np.stack([_tile_w(f(inp["w_o"][l])) for l in range(DEPTH)])
    sh["w_f1"] = np.stack([_tile_w(f(inp["w_ffn_in"][l])) for l in range(DEPTH)])
    sh["w_f2"] = np.stack([_tile_w(f(inp["w_ffn_out"][l])) for l in range(DEPTH)])
    vecs = np.zeros((DEPTH, 128, c.NV), np.float32)
    for l in range(DEPTH):
        def put(nm, arr):
            o = c.VO[nm]
            vecs[l, :, o:o + arr.shape[1]] = arr
        put("ada_b", _pcol(f(inp["ada_b"][l])))
        put("n1g", _pcol(f(inp["norm1_g"][l])))
        put("n2g", _pcol(f(inp["norm2_g"][l])))
        put("b_in", _pcol(f(inp["b_in"][l])))
        cw = f(inp["conv_w"][l])
        put("cw", np.concatenate([_pcol(cw[k]) for k in range(4)], axis=1))
        put("cb", _pcol(f(inp["conv_b"][l])))
        put("ba", _pcol(f(inp["lru_ba"][l])))
        put("bx", _pcol(f(inp["lru_bx"][l])))
        put("lam", _pcol(f(inp["lru_lambda"][l])))
        put("sink", np.broadcast_to(f(inp["sinks"][l])[None, :], (128, c.NQ)))
        put("fg", _pcol(f(inp["final_g"])))
    sh["vecs"] = vecs
    return sh


def prep_core(cfg, inp, r):
    c = cfg
    b, half = r // c.SPLIT, r % c.SPLIT
    x = np.asarray(inp["x"][b], dtype=np.float32)[half * c.TOK:(half + 1) * c.TOK]
    xT = np.ascontiguousarray(x.T.reshape(c.KC, 128, c.TOK).transpose(1, 0, 2))
    cv = _pcol(np.asarray(inp["c"][b], dtype=np.float32))
    fl = np.full((128, 1), float(half), np.float32)
    return {"xin": xT, "cvec": cv, "flag": fl}


_NC_CACHE = {}


def run_cfg(cfg, inp):
    key = (cfg.D, cfg.NQ, cfg.NKV, cfg.FH, cfg.SEQ, cfg.DEPTH, cfg.T, cfg.SPLIT, cfg.BATCH)
    if key not in _NC_CACHE:
        _NC_CACHE[key] = build_program(cfg)
    nc = _NC_CACHE[key]
    sh = prep_shared(cfg, inp)
    in_maps = []
    for r in range(cfg.NCORES):
        m = dict(sh)
        m.update(prep_core(cfg, inp, r))
        in_maps.append(m)
    res = run_bass_kernel_spmd(nc, in_maps, core_ids=list(range(cfg.NCORES)))
    out = np.empty((cfg.BATCH, cfg.SEQ, cfg.D), np.float32)
    for r in range(cfg.NCORES):
        b, half = r // cfg.SPLIT, r % cfg.SPLIT
        y = np.asarray(res.results[r]["y"])
        out[b, half * cfg.TOK:(half + 1) * cfg.TOK, :] = y.transpose(1, 0, 2).reshape(cfg.D, cfg.TOK).T
    return out


def kernel(**inputs):
    cfg = Cfg()
    return run_cfg(cfg, inputs)
```

```python
import math
from contextlib import ExitStack

import numpy as np
import concourse.bass as bass
import concourse.mybir as mybir
from concourse.bass_utils import run_bass_kernel_spmd

F32 = mybir.dt.float32
BF16 = mybir.dt.bfloat16
AF = mybir.ActivationFunctionType
ALU = mybir.AluOpType

EPS = 1e-6
LRU_C = 8.0


class Cfg:
    def __init__(self, D=2048, NQ=16, NKV=4, FH=5632, SEQ=4096, BATCH=4, DEPTH=2, T=1024, SPLIT=2):
        self.D, self.NQ, self.NKV, self.FH, self.SEQ, self.BATCH, self.DEPTH, self.T = D, NQ, NKV, FH, SEQ, BATCH, DEPTH, T
        self.SPLIT = SPLIT
        self.NCORES = BATCH * SPLIT
        self.KC = D // 128
        self.G = NQ // NKV
        self.AWC = NQ
        self.FC = FH // 128
        self.TOK = SEQ // SPLIT
        self.NT = self.TOK // T
        self.NS = T // 512
        self.NB = T // 128
        self.CIN = 2 * D + NQ * 128 + 2 * NKV * 128 + 2 * D
        self.MC = self.CIN // 128
        KC = self.KC
        self.U0, self.GT0, self.Q0 = 0, KC, 2 * KC
        self.K0 = self.Q0 + NQ
        self.V0 = self.K0 + NKV
        self.GA0 = self.V0 + NKV
        self.GB0 = self.GA0 + KC
        o = {}
        p = 0
        for name, n in [("ada_b", 6 * KC), ("n1g", KC), ("n2g", KC), ("b_in", self.MC), ("cw", 4 * KC), ("cb", KC),
                        ("ba", KC), ("bx", KC), ("lam", KC), ("sink", NQ), ("fg", KC)]:
            o[name] = p
            p += n
        self.VO, self.NV = o, p
        d = {}
        p = 0
        for name, n in [("mod", 6 * KC), ("s1", KC), ("s2", KC), ("nc8", KC), ("nc16", KC), ("hnc8", KC), ("hba", KC), ("hbx", KC), ("z", KC), ("t", KC), ("esk", NQ)]:
            d[name] = p
            p += n
        self.DO, self.NDV = d, p


class Sch:
    def __init__(self, nc, es):
        self.nc = nc
        self.es = es
        self.prog = {k: [] for k in ("pe", "act", "dve", "pool", "sp")}
        self.semh = {}
        self.cnt = {}
        for k in ("pe", "act", "dve", "pool"):
            self.semh[k] = es.enter_context(nc.semaphore("s_" + k))
            self.cnt[k] = 0
        self.clock = {k: {} for k in self.prog}
        self.snap = {}
        self.lastw = {}
        self.readers = {}
        self.nwait = 0

    def slot(self, name):
        if name not in self.semh:
            self.semh[name] = self.es.enter_context(self.nc.semaphore("d_" + name))
            self.cnt[name] = 0
        return name

    def _deps(self, eng, reads, writes):
        deps = {}

        def add(k):
            if k is None:
                return
            f, n = k
            if deps.get(f, 0) < n:
                deps[f] = n

        for r in reads:
            add(self.lastw.get(r))
            if r[0] == "ps":
                for f, n in self.readers.get(r, {}).items():
                    if f != eng:
                        add((f, n))
        for w in writes:
            add(self.lastw.get(w))
            for f, n in self.readers.get(w, {}).items():
                if f == eng and eng == "pe":
                    continue
                add((f, n))
        return deps

    def _emit_waits(self, eng, deps):
        ck = self.clock[eng]
        for f, n in deps.items():
            if f == eng and eng == "pe":
                continue
            if ck.get(f, 0) >= n:
                continue
            assert n <= self.cnt[f], ("dependency on unmaterialised count", eng, f, n, self.cnt[f])
            h = self.semh[f]
            self.prog[eng].append(("w", h, n))
            self.nwait += 1
            sn = self.snap.get((f, n))
            if sn:
                for a, b in sn.items():
                    if ck.get(a, 0) < b:
                        ck[a] = b
            ck[f] = max(ck.get(f, 0), n)

    def _record(self, key, reads, writes):
        for r in reads:
            self.readers.setdefault(r, {})[key[0]] = key[1]
        for w in writes:
            self.lastw[w] = key
            self.readers[w] = {}

    def op(self, eng, fn, reads=(), writes=(), inc=True):
        deps = self._deps(eng, reads, writes)
        self._emit_waits(eng, deps)
        n = self.cnt[eng] + 1
        if inc:
            self.cnt[eng] = n
            self.prog[eng].append(("i", fn, self.semh[eng], 1))
            sn = dict(self.clock[eng])
            sn[eng] = n
            self.snap[(eng, n)] = sn
            self.clock[eng][eng] = max(self.clock[eng].get(eng, 0), 0)
        else:
            self.prog[eng].append(("i", fn, None, 0))
        self._record((eng, n), reads, writes)

    def dma(self, q, slot, fn, reads=(), writes=(), inc=16):
        self.slot(slot)
        deps = self._deps(q, reads, writes)
        if self.cnt[slot] > 0:
            deps[slot] = max(deps.get(slot, 0), self.cnt[slot])
        self._emit_waits(q, deps)
        n = self.cnt[slot] + inc
        self.cnt[slot] = n
        self.prog[q].append(("i", fn, self.semh[slot], inc))
        self.snap[(slot, n)] = dict(self.clock[q])
        self._record((slot, n), reads, writes)

    def final_wait(self, eng, slots):
        for s in slots:
            if self.cnt.get(s, 0) > 0:
                self.prog[eng].append(("w", self.semh[s], self.cnt[s]))

    def emit(self):
        nc = self.nc
        prog = self.prog

        def run(e, lst):
            for it in lst:
                if it[0] == "w":
                    e.wait_ge(it[1], it[2])
                else:
                    ins = it[1](e)
                    if it[2] is not None:
                        ins.then_inc(it[2], it[3])

        with nc.Block() as block:
            @block.tensor
            def _(e):
                run(e, prog["pe"])

            @block.scalar
            def _(e):
                run(e, prog["act"])

            @block.vector
            def _(e):
                run(e, prog["dve"])

            @block.gpsimd
            def _(e):
                run(e, prog["pool"])

            @block.sync
            def _(e):
                run(e, prog["sp"])


def blk_res(name, lo, hi, blk=1024):
    return [(name, b) for b in range(lo // blk, (hi - 1) // blk + 1)]


def build_program(cfg):
    c = cfg
    D, KC, T, NS, NB, NT, TOK, FC, G, NQ, NKV, AWC = c.D, c.KC, c.T, c.NS, c.NB, c.NT, c.TOK, c.FC, c.G, c.NQ, c.NKV, c.AWC
    DEPTH = c.DEPTH
    nc = bass.Bass("TRN2", target_bir_lowering=False)
    es = ExitStack()

    def din(name, shape, dt=F32):
        return nc.dram_tensor(name, list(shape), dt, kind="ExternalInput").ap()

    xin = din("xin", [128, KC, TOK])
    cact_in = din("cvec", [128, KC])
    vecs_in = din("vecs", [DEPTH, 128, c.NV])
    ada_w = din("ada_w", [DEPTH, 6 * KC, 128, KC * 128])
    w_in = din("w_in", [DEPTH, c.MC, 128, KC * 128])
    lru_wa = din("lru_wa", [DEPTH, 128, KC * 128])
    lru_wx = din("lru_wx", [DEPTH, 128, KC * 128])
    w_lo = din("w_lo", [DEPTH, KC, 128, KC * 128])
    w_ao = din("w_ao", [DEPTH, KC, 128, AWC * 128])
    w_o = din("w_o", [DEPTH, KC, 128, KC * 128])
    w_f1 = din("w_f1", [DEPTH, 2 * FC, 128, KC * 128])
    w_f2 = din("w_f2", [DEPTH, KC, 128, FC * 128])
    yout = nc.dram_tensor("y", [128, KC, TOK], F32, kind="ExternalOutput").ap()
    xs = nc.dram_tensor("xs", [128, KC, TOK], F32, kind="Internal").ap()
    flag_in = din("flag", [128, 1])
    snd = [nc.dram_tensor("snd%d" % l, [128, 5 * KC + 2 * NKV * 128], F32).ap() for l in range(DEPTH)]
    rcv = [nc.dram_tensor("rcv%d" % l, [256, 5 * KC + 2 * NKV * 128], F32).ap() for l in range(DEPTH)]

    def sb(name, shape, dt):
        return es.enter_context(nc.sbuf_tensor(name, list(shape), dt))

    S = Sch(nc, es)

    ASZ = (3 * KC + AWC) * T
    arena = sb("arena", [128, ASZ], BF16)
    o_H, o_YL, o_MG, o_YA = 0, KC * T, 2 * KC * T, 3 * KC * T

    def aview(off, nchunk, dt=BF16):
        n = nchunk * T * (2 if dt == F32 else 1)
        v = arena[:, off:off + n]
        if dt == F32:
            v = v.bitcast(F32)
        return v.rearrange("p (c t) -> p c t", t=T)

    def ares(off, ci, t0, t1, dt=BF16):
        esz = 4 if dt == F32 else 2
        lo = off * 2 + (ci * T + t0) * esz
        hi = off * 2 + (ci * T + t1) * esz
        return blk_res("A", lo, hi)

    H = aview(o_H, KC)
    YL = aview(o_YL, KC)
    MG = aview(o_MG, KC)
    YA = aview(o_YA, AWC)
    XT = aview(o_YL, KC, F32)
    X2 = aview(o_H, KC, F32)
    H2 = aview(o_YA, KC)
    HID = aview(o_H, FC)
    assert FC <= 3 * KC and AWC >= KC

    NU = 7 * 1024 + 512
    LRU_SZ = 2 * NU + 2 * (T + 8)
    ATT_SZ = G * T + (128 + T) + T + (NB + 1) * 128 + 4 * 512 * 2 + 2 * 1024 * 2
    SSZ = max(LRU_SZ, ATT_SZ)
    scr = sb("scr", [128, SSZ], BF16)

    def sres(off, n):
        return blk_res("S", off * 2, (off + n) * 2, blk=256)

    def sview(off, n, dt=BF16):
        m = n * (2 if dt == F32 else 1)
        v = scr[:, off:off + m]
        return (v.bitcast(F32) if dt == F32 else v), sres(off, m)

    lru_sets = []
    p = 0
    LNAMES = ("G", "UC", "R1", "A2", "I1", "HL", "GL")
    for u in range(2):
        st = {}
        for nm in LNAMES:
            st[nm] = sview(p, 512, F32)
            p += 1024
        st["UCB"] = sview(p, 512, BF16)
        p += 512
        lru_sets.append(st)
    UB = []
    for u in range(2):
        UB.append(sview(p, T + 8, BF16))
        p += (T + 8)
    if KC * T >= 2 * NU:
        xplaces = [(arena, "A", o_MG, 1024), (arena, "A", o_YA, 1024)]
    else:
        xten = sb("xtra", [128, 4 * NU], BF16)
        xplaces = [(xten, "X", 0, 256), (xten, "X", 2 * NU, 256)]
    for (xten_, xname, xbase, xblk) in xplaces:
        q = xbase
        for u in range(2):
            st = {}
            for nm in LNAMES + ("UCB",):
                n_ = 512 if nm == "UCB" else 1024
                v = xten_[:, q:q + n_]
                st[nm] = ((v if nm == "UCB" else v.bitcast(F32)), blk_res(xname, q * 2, (q + n_) * 2, blk=xblk))
                q += n_
            lru_sets.append(st)
    p = 0
    oQG = p
    QG_ap, QG_res = sview(p, G * T); p += G * T
    QG = QG_ap.rearrange("p (g t) -> p g t", t=T)
    oKT = p
    KT, KT_res = sview(p, 128 + T); p += 128 + T
    oVT = p
    VT, VT_res = sview(p, T); p += T
    oVTOK = p
    VTOK_ap, VTOK_res = sview(p, (NB + 1) * 128); p += (NB + 1) * 128
    qgr = lambda j, t0, t1: sres(oQG + j * T + t0, t1 - t0)
    ktr = lambda a, b: sres(oKT + a, b - a)
    vtr = lambda a, b: sres(oVT + a, b - a)
    vkr = lambda b0, b1: sres(oVTOK + b0 * 128, (b1 - b0) * 128)
    VTOK = VTOK_ap.rearrange("p (b d) -> p b d", d=128)
    att_sets = []
    for u in range(2):
        st = {}
        for nm in ("ERP", "ERC", "EP", "EC"):
            st[nm] = sview(p, 512); p += 512
        att_sets.append(st)
    for u in range(2):
        att_sets[u]["DEN"] = sview(p, 512, F32); p += 1024
    assert p <= SSZ

    vecs = [sb("vecs%d" % l, [128, c.NV], F32) for l in range(DEPTH)]
    dv = [sb("dv%d" % l, [128, c.NDV], F32) for l in range(DEPTH)]
    cact32 = sb("cact32", [128, KC], F32)
    cact = sb("cact", [128, KC], BF16)
    ones_bf = sb("ones_bf", [128, 128], BF16)
    ident_bf = sb("ident_bf", [128, 128], BF16)
    _a, identf_r = sview(0, 128, F32)
    ident_f = _a
    _a, mcurf_r = sview(1024, G * 128, F32)
    mcur_f = _a.rearrange("p (g q) -> p g q", g=G)
    _a, mprevf_r = sview(1024 + 2 * G * 128, G * 128, F32)
    mprev_f = _a.rearrange("p (g q) -> p g q", g=G)
    mcur = sb("mcur", [128, G * 128], BF16)
    mprev0 = sb("mprev0", [128, G * 128], BF16)
    flag = sb("flag_sb", [128, 1], F32)
    WX = 5 * KC + 2 * NKV * 128
    mprev = sb("mprev", [128, G * 128], BF16)
    lstate = sb("lstate", [128, KC], F32)
    carry_u = sb("carry_u", [128, KC, 4], BF16)
    DG = [sb("dg%d" % i, [128, 4, 128], BF16) for i in range(2)]
    carry_k = sb("carry_k", [128, NKV, 128], BF16)
    carry_v = sb("carry_v", [128, NKV, 128], BF16)
    SQ = [lru_sets[i]["UCB"] for i in range(2)]
    TMPN = [lru_sets[i]["UC"] for i in range(2)]
    RS, RSr = lru_sets[0]["R1"]
    RI, RIr = lru_sets[0]["A2"]
    XCB = [sb("xcb%d" % i, [128, 512], F32) for i in range(2)]
    NRING = 6
    ring = [sb("ring%d" % i, [128, 16 * 128], BF16) for i in range(NRING)]
    ps = es.enter_context(nc.psum_tensor("ps", [128, 8, 512], F32))

    st_ring = {"i": 0}
    st_ps = {"i": 0}

    st_ps["n"] = 8

    def pbank(fixed=None):
        if fixed is not None:
            return ps[:, fixed, :], [("ps", fixed)]
        i = st_ps["i"] % st_ps["n"]
        st_ps["i"] += 1
        return ps[:, i, :], [("ps", i)]

    def wload(src2d, kcn):
        i = st_ring["i"] % NRING
        st_ring["i"] += 1
        dst = ring[i][:, 0:kcn * 128]
        res = [("ring", i)]
        S.dma("pool", "ring%d" % i, lambda e, dst=dst, src=src2d: e.dma_start(out=dst, in_=src), reads=(), writes=res)
        return dst.rearrange("p (k m) -> p k m", m=128), res

    def mm_acc(pb, pres, pairs, extra_reads=()):
        n = len(pairs)
        for i, (l, r, rd) in enumerate(pairs):
            S.op("pe", lambda e, l=l, r=r, i=i: e.matmul(pb, lhsT=l, rhs=r, start=(i == 0), stop=(i == n - 1)),
                 reads=list(rd) + list(extra_reads), writes=pres, inc=(i == n - 1))

    def act(out, in_, func, reads, writes, bias=None, scale=None):
        kw = {}
        if bias is not None:
            kw["bias"] = bias
        if scale is not None:
            kw["scale"] = scale
        S.op("act", lambda e: e.activation(out=out, in_=in_, func=func, **kw), reads=reads, writes=writes)

    VR = lambda l: [("vecs", l)]
    DR = lambda l, nm: [("dv", l, nm)]
    S.op("dve", lambda e: e.memset(ones_bf[:], 1.0), writes=[("ones",)])
    S.op("pool", lambda e: e.memset(ident_f, 1.0), writes=identf_r)
    S.op("pool", lambda e: e.affine_select(out=ident_f, in_=ident_f, compare_op=ALU.is_equal, fill=0.0, base=0,
                                           pattern=[[-1, 128]], channel_multiplier=1),
         reads=identf_r, writes=identf_r)
    S.op("pool", lambda e: e.memset(mcur_f, 1.0), writes=mcurf_r)
    S.op("pool", lambda e: e.affine_select(out=mcur_f, in_=mcur_f, compare_op=ALU.is_ge, fill=0.0, base=0,
                                           pattern=[[0, G], [1, 128]], channel_multiplier=-1),
         reads=mcurf_r, writes=mcurf_r)
    S.op("pool", lambda e: e.memset(mprev_f, 1.0), writes=mprevf_r)
    S.op("pool", lambda e: e.affine_select(out=mprev_f, in_=mprev_f, compare_op=ALU.is_gt, fill=0.0, base=0,
                                           pattern=[[0, G], [-1, 128]], channel_multiplier=1),
         reads=mprevf_r, writes=mprevf_r)
    S.op("dve", lambda e: e.tensor_copy(out=ident_bf[:], in_=ident_f), reads=identf_r, writes=[("ident",)])
    S.op("dve", lambda e: e.tensor_copy(out=mcur[:].rearrange("p (g q) -> p g q", g=G), in_=mcur_f), reads=mcurf_r, writes=[("mcur",)])
    S.op("dve", lambda e: e.tensor_copy(out=mprev[:].rearrange("p (g q) -> p g q", g=G), in_=mprev_f), reads=mprevf_r, writes=[("mprev",)])
    S.dma("sp", "ld_c", lambda e: e.dma_start(out=cact32[:], in_=cact_in[:, :]), writes=[("cact32",)])
    S.dma("sp", "ld_f", lambda e: e.dma_start(out=flag[:], in_=flag_in[:, :]), writes=[("flag",)])
    S.op("dve", lambda e: e.tensor_scalar(out=mprev0[:], in0=mprev[:], scalar1=flag[:, 0:1], scalar2=None, op0=ALU.mult),
         reads=[("mprev",), ("flag",)], writes=[("mprev0",)])
    for l in range(DEPTH):
        S.dma("sp", "ld_v%d" % l, lambda e, l=l: e.dma_start(out=vecs[l][:], in_=vecs_in[l, :, :]), writes=VR(l))
    act(cact[:], cact32[:], AF.Silu, [("cact32",)], [("cact",)])

    def vcol(l, nm, i, n=1):
        o = c.VO[nm] + i
        return vecs[l][:, o:o + n]

    def dcol(l, nm, i, n=1):
        o = c.DO[nm] + i
        return dv[l][:, o:o + n]

    ada_q = [(l_, j_) for l_ in range(DEPTH) for j_ in range(6 * KC)]
    ada_state = {"derived": set()}

    def DRm(l, part):
        return [("dv", l, "mod", part)]

    def MODR(l):
        return [("dv", l, "mod", p_) for p_ in range(6)]

    def ada_job(l, j):
        w, wres = wload(ada_w[l, j, :, :], KC)
        pb, pres = pbank()
        for kc in range(KC):
            S.op("pe", lambda e, kc=kc, w=w, pb=pb: e.matmul(pb[:, 0:1], lhsT=w[:, kc, :], rhs=cact[:, kc:kc + 1],
                                                         start=(kc == 0), stop=(kc == KC - 1)),
                 reads=wres + [("cact",)], writes=pres, inc=(kc == KC - 1))
        S.op("dve", lambda e, pb=pb: e.tensor_tensor(out=dcol(l, "mod", j), in0=pb[:, 0:1], in1=vcol(l, "ada_b", j), op=ALU.add),
             reads=pres + VR(l), writes=DRm(l, j // KC))

    def ada_tick(n=1):
        for _ in range(n):
            if ada_q:
                ada_job(*ada_q.pop(0))

    def ada_need(l, part):
        while ada_q and ada_q[0] <= (l, (part + 1) * KC - 1):
            ada_job(*ada_q.pop(0))
        if part >= 1 and (l, "s1") not in ada_state["derived"]:
            ada_state["derived"].add((l, "s1"))
            S.op("dve", lambda e: e.scalar_tensor_tensor(out=dcol(l, "s1", 0, KC), in0=dcol(l, "mod", KC, KC), scalar=1.0,
                                                         in1=vcol(l, "n1g", 0, KC), op0=ALU.add, op1=ALU.mult),
                 reads=DRm(l, 1) + VR(l), writes=DR(l, "s1"))
        if part >= 4 and (l, "s2") not in ada_state["derived"]:
            ada_state["derived"].add((l, "s2"))
            S.op("dve", lambda e: e.scalar_tensor_tensor(out=dcol(l, "s2", 0, KC), in0=dcol(l, "mod", 4 * KC, KC), scalar=1.0,
                                                         in1=vcol(l, "n2g", 0, KC), op0=ALU.add, op1=ALU.mult),
                 reads=DRm(l, 4) + VR(l), writes=DR(l, "s2"))

    def layer_setup(l):
        zc, tc_ = dcol(l, "z", 0, KC), dcol(l, "t", 0, KC)
        act(zc, vcol(l, "lam", 0, KC), AF.Exp, VR(l), DR(l, "z"), scale=-1.0)
        S.op("dve", lambda e: e.tensor_scalar(out=tc_, in0=zc, scalar1=0.2, scalar2=-0.25, op0=ALU.mult, op1=ALU.add),
             reads=DR(l, "z"), writes=DR(l, "t"))
        for cst in (1.0 / 3.0, -0.5, 1.0):
            S.op("dve", lambda e: e.tensor_tensor(out=tc_, in0=tc_, in1=zc, op=ALU.mult), reads=DR(l, "z") + DR(l, "t"), writes=DR(l, "t"))
            S.op("dve", lambda e, cst=cst: e.tensor_scalar(out=tc_, in0=tc_, scalar1=cst, scalar2=None, op0=ALU.add),
                 reads=DR(l, "t"), writes=DR(l, "t"))
        S.op("dve", lambda e: e.scalar_tensor_tensor(out=dcol(l, "nc8", 0, KC), in0=tc_, scalar=-LRU_C, in1=zc, op0=ALU.mult, op1=ALU.mult),
             reads=DR(l, "z") + DR(l, "t"), writes=DR(l, "nc8"))
        S.op("dve", lambda e: e.tensor_scalar(out=dcol(l, "nc16", 0, KC), in0=dcol(l, "nc8", 0, KC), scalar1=2.0, scalar2=None, op0=ALU.mult),
             reads=DR(l, "nc8"), writes=DR(l, "nc16"))
        act(dcol(l, "esk", 0, NQ), vcol(l, "sink", 0, NQ), AF.Exp, VR(l), DR(l, "esk"))
        S.op("dve", lambda e: e.tensor_scalar(out=dcol(l, "hnc8", 0, KC), in0=dcol(l, "nc8", 0, KC), scalar1=0.5, scalar2=None, op0=ALU.mult),
             reads=DR(l, "nc8"), writes=DR(l, "hnc8"))
        S.op("dve", lambda e: e.tensor_scalar(out=dcol(l, "hba", 0, KC), in0=vcol(l, "ba", 0, KC), scalar1=0.5, scalar2=None, op0=ALU.mult),
             reads=VR(l), writes=DR(l, "hba"))
        S.op("dve", lambda e: e.tensor_scalar(out=dcol(l, "hbx", 0, KC), in0=vcol(l, "bx", 0, KC), scalar1=0.5, scalar2=None, op0=ALU.mult),
             reads=VR(l), writes=DR(l, "hbx"))

    def layer_state_reset(l):
        S.op("dve", lambda e: e.memset(lstate[:], 0.0), writes=[("lstate", i) for i in range(KC)])
        S.op("dve", lambda e: e.memset(carry_u[:], 0.0), writes=[("carry_u", i) for i in range(KC)])
        S.op("dve", lambda e: e.memset(carry_k[:], 0.0), writes=[("carry_k", i) for i in range(NKV)])
        S.op("dve", lambda e: e.memset(carry_v[:], 0.0), writes=[("carry_v", i) for i in range(NKV)])

    def norm_stats(XV, x_off, t0, t1):
        pb, pres = pbank()
        for kc in range(KC):
            sq, sqr = SQ[kc % 2]
            xr = ares(x_off, kc, t0, t1, F32)
            act(sq, XV[:, kc, t0:t1], AF.Square, xr, sqr)
            S.op("pe", lambda e, sq=sq, kc=kc: e.matmul(pb, lhsT=ones_bf[:], rhs=sq, start=(kc == 0), stop=(kc == KC - 1)),
                 reads=sqr + [("ones",)], writes=pres, inc=True)
        act(RS, pb, AF.Sqrt, pres + [("eps",)], RSr, bias=EPS_AP[:, 0:1], scale=1.0 / D)
        S.op("dve", lambda e: e.reciprocal(out=RI, in_=RS), reads=RSr, writes=RIr)

    def norm_mod(l, XV, x_off, OUT, out_off, scol, bcol, mpart):
        for ns in range(NS):
            t0, t1 = ns * 512, (ns + 1) * 512
            norm_stats(XV, x_off, t0, t1)
            for kc in range(KC):
                tm, tmr = TMPN[kc % 2]
                xr = ares(x_off, kc, t0, t1, F32)
                S.op("dve", lambda e, tm=tm, kc=kc, t0=t0, t1=t1: e.scalar_tensor_tensor(out=tm, in0=XV[:, kc, t0:t1], scalar=scol(kc), in1=RI,
                                                                                     op0=ALU.mult, op1=ALU.mult),
                     reads=xr + RIr + DR(l, "s1") + DR(l, "s2"), writes=tmr)
                act(OUT[:, kc, t0:t1], tm, AF.Identity, tmr + DRm(l, mpart), ares(out_off, kc, t0, t1), bias=bcol(kc))

    EPS_AP = sb("eps_ap", [128, 1], F32)
    S.op("dve", lambda e: e.memset(EPS_AP[:], EPS), writes=[("eps",)])

    def proj(w, wres, RHS, rhs_off, nk, ns):
        t0, t1 = ns * 512, (ns + 1) * 512
        pb, pres = pbank()
        mm_acc(pb, pres, [(w[:, k, :], RHS[:, k, t0:t1], wres + ares(rhs_off, k, t0, t1)) for k in range(nk)])
        return pb, pres

    def load_norm1(l, tt):
        tok0 = tt * T
        xsrc = xin if l == 0 else xs
        xsrc_name = "xin" if l == 0 else "xs"
        for kc in range(KC):
            S.dma("sp", "ld_xt%d" % (kc % 4), lambda e, kc=kc: e.dma_start(out=XT[:, kc, :], in_=xsrc[:, kc, tok0:tok0 + T]),
                  reads=[(xsrc_name, kc, tt, n_) for n_ in range(NS)], writes=ares(o_YL, kc, 0, T, F32))
        ada_need(l, 1)
        norm_mod(l, XT, o_YL, H, o_H, lambda kc: dcol(l, "s1", kc), lambda kc: dcol(l, "mod", 0 * KC + kc), 0)

    def lru_branch(l, state_only):
        st_ps["n"] = 4
        st_ps["i"] = 0
        GC0 = math.sqrt(0.044715)
        GC1 = 0.7978845608028654
        pend = {"q": None, "p2": None, "dve1": None, "dve2": None}

        def pre(ch):
            cx = {}
            ub, _ = UB[ch % 2]
            cx["ub"], cx["ubname"], cx["dg"] = ub, "ubh%d" % (ch % 2), DG[ch % 2]
            dg = cx["dg"]
            S.op("dve", lambda e, ub=ub, ch=ch: e.tensor_copy(out=ub[:, 1:4], in_=carry_u[:, ch, 0:3]),
                 reads=[("carry_u", ch)], writes=[(cx["ubname"],)])
            for k in range(4):
                S.op("dve", lambda e, dg=dg, k=k, ch=ch: e.tensor_scalar(out=dg[:, k, :], in0=ident_bf[:], scalar1=vcol(l, "cw", k * KC + ch),
                                                                       scalar2=None, op0=ALU.mult),
                     reads=[("ident",)] + VR(l), writes=[("dg", ch % 2, k)])
            cx["ubr"] = [[("ub", ch % 2, ns)] for ns in range(NS)]
            return cx

        def pre_b(ch, cx):
            ada_tick(2)
            if not state_only:
                cx["wg"] = wload(w_in[l, c.GT0 + ch, :, :], KC)
            cx["wa"] = wload(lru_wa[l, :, ch * 128:(ch + 1) * 128], 1)
            cx["wx"] = wload(lru_wx[l, :, ch * 128:(ch + 1) * 128], 1)

        def uproj(ch, ns, cx):
            ub = cx["ub"]
            if "wu" not in cx:
                cx["wu"] = wload(w_in[l, c.U0 + ch, :, :], KC)
            wu, wures = cx["wu"]
            pb, pres = proj(wu, wures, H, o_H, KC, ns)
            S.op("dve", lambda e, ub=ub, pb=pb, ns=ns, ch=ch: e.tensor_scalar(out=ub[:, 4 + ns * 512:4 + (ns + 1) * 512], in0=pb,
                                                                             scalar1=vcol(l, "b_in", c.U0 + ch), scalar2=None, op0=ALU.add),
                 reads=pres + VR(l), writes=cx["ubr"][ns])
            if ns == NS - 1:
                S.op("dve", lambda e, ub=ub, ch=ch: e.tensor_copy(out=carry_u[:, ch, 0:3], in_=ub[:, T + 1:T + 4]),
                     reads=cx["ubr"][NS - 1], writes=[("carry_u", ch)])

        def gproj(ch, ns, cx, st):
            Gv, Gr = st["G"]
            wg, wgres = cx["wg"]
            pb, pres = proj(wg, wgres, H, o_H, KC, ns)
            S.op("dve", lambda e, Gv=Gv, pb=pb, ch=ch: e.tensor_scalar(out=Gv, in0=pb, scalar1=vcol(l, "b_in", c.GT0 + ch), scalar2=None, op0=ALU.add),
                 reads=pres + VR(l), writes=Gr)

        def conv(ch, ns, cx, st):
            ub, dg, ubr = cx["ub"], cx["dg"], cx["ubr"]
            UCB, UCBr = st["UCB"]
            o = 4 + ns * 512
            ubrd = ubr[ns] + ([(cx["ubname"],)] if ns == 0 else ubr[ns - 1])
            pcv, pcvres = pbank()
            for k in range(4):
                j = 3 - k
                S.op("pe", lambda e, pcv=pcv, dg=dg, ub=ub, k=k, j=j, o=o: e.matmul(pcv, lhsT=dg[:, k, :], rhs=ub[:, o - j:o - j + 512],
                                                                               start=(k == 0), stop=(k == 3)),
                     reads=[("dg", ch % 2, k)] + ubrd, writes=pcvres, inc=(k == 3))
            S.op("dve", lambda e, UCB=UCB, pcv=pcv, ch=ch: e.tensor_scalar(out=UCB, in0=pcv, scalar1=vcol(l, "cb", ch), scalar2=None, op0=ALU.add),
                 reads=pcvres + VR(l), writes=UCBr)

        def gates(ch, ns, cx, st):
            UCB, UCBr = st["UCB"]
            wa_c, wares = cx["wa"]
            wx_c, wxres = cx["wx"]
            pa, pares = pbank(fixed=4 + 2 * (ns % 2))
            S.op("pe", lambda e, pa=pa, wa_c=wa_c, UCB=UCB: e.matmul(pa, lhsT=wa_c[:, 0, :], rhs=UCB, start=True, stop=True),
                 reads=wares + UCBr, writes=pares)
            px, pxres = pbank(fixed=5 + 2 * (ns % 2))
            S.op("pe", lambda e, px=px, wx_c=wx_c, UCB=UCB: e.matmul(px, lhsT=wx_c[:, 0, :], rhs=UCB, start=True, stop=True),
                 reads=wxres + UCBr, writes=pxres)
            return (pa, pares, px, pxres)

        assert NS == 2
        cx = pre(0)
        uproj(0, 0, cx)
        uproj(0, 1, cx)
        for ch in range(KC):
            pre_b(ch, cx)
            nxt = pre(ch + 1) if ch + 1 < KC else None
            sets = [lru_sets[2 * (ch % 3) + ns] for ns in range(NS)]
            banks = [None, None]
            if not state_only:
                gproj(ch, 0, cx, sets[0])
            conv(ch, 0, cx, sets[0])
            if pend["q"] is not None:
                pend["q"]()
            if pend["dve2"] is not None:
                pend["dve2"](0)
            if not state_only:
                gproj(ch, 1, cx, sets[1])
            elif nxt is not None:
                uproj(ch + 1, 0, nxt)
            banks[0] = gates(ch, 0, cx, sets[0])
            if pend["p2"] is not None:
                pend["p2"]()
            conv(ch, 1, cx, sets[1])
            if pend["dve2"] is not None:
                pend["dve2"](1)
            if nxt is not None:
                uproj(ch + 1, 0 if not state_only else 1, nxt)
            banks[1] = gates(ch, 1, cx, sets[1])
            if nxt is not None and not state_only:
                uproj(ch + 1, 1, nxt)
            cx = nxt
            for ns in range(NS):
                st = sets[ns]
                pa, pares, px, pxres = banks[ns]
                R1, R1r = st["R1"]; I1, I1r = st["I1"]
                act(R1, pa, AF.Tanh, pares + DR(l, "hba"), R1r, bias=dcol(l, "hba", ch), scale=0.5)
                act(I1, px, AF.Tanh, pxres + DR(l, "hbx"), I1r, bias=dcol(l, "hbx", ch), scale=0.5)
            for ns in range(NS):
                st = sets[ns]
                R1, R1r = st["R1"]; A2, A2r = st["A2"]
                act(A2, R1, AF.Exp, R1r + DR(l, "nc8"), A2r, bias=dcol(l, "nc8", ch), scale=dcol(l, "nc8", ch))
                act(R1, R1, AF.Exp, R1r + DR(l, "hnc8"), R1r, bias=dcol(l, "hnc8", ch), scale=dcol(l, "hnc8", ch))
            if not state_only:
                for ns in range(NS):
                    st = sets[ns]
                    Gv, Gr = st["G"]; GL, GLr = st["GL"]
                    act(GL, Gv, AF.Square, Gr, GLr, scale=GC0)

            def q_dve(sets=sets):
                if state_only:
                    return
                for ns in range(NS):
                    Gv, Gr = sets[ns]["G"]; GL, GLr = sets[ns]["GL"]
                    S.op("dve", lambda e, GL=GL, Gv=Gv: e.scalar_tensor_tensor(out=GL, in0=GL, scalar=1.0, in1=Gv, op0=ALU.add, op1=ALU.mult),
                         reads=GLr + Gr, writes=GLr)

            def part2(sets=sets):
                if not state_only:
                    for ns in range(NS):
                        GL, GLr = sets[ns]["GL"]
                        act(GL, GL, AF.Tanh, GLr, GLr, scale=GC1)
                for ns in range(NS):
                    A2, A2r = sets[ns]["A2"]
                    act(A2, A2, AF.Sqrt, A2r + [("one",)], A2r, bias=ONE_AP[:, 0:1], scale=-1.0)

            def back_dve(ns, ch=ch, sets=sets):
                st = sets[ns]
                Gv, Gr = st["G"]; R1, R1r = st["R1"]; A2, A2r = st["A2"]; UCB, UCBr = st["UCB"]
                I1, I1r = st["I1"]; HL, HLr = st["HL"]; GL, GLr = st["GL"]
                S.op("dve", lambda e: e.scalar_tensor_tensor(out=I1, in0=I1, scalar=1.0, in1=UCB, op0=ALU.add, op1=ALU.mult),
                     reads=I1r + UCBr, writes=I1r)
                S.op("dve", lambda e: e.scalar_tensor_tensor(out=I1, in0=I1, scalar=0.5, in1=A2, op0=ALU.mult, op1=ALU.mult),
                     reads=I1r + A2r, writes=I1r)
                if ns == 0:
                    init, initr = lstate[:, ch:ch + 1], [("lstate", ch)]
                else:
                    init, initr = sets[ns - 1]["HL"][0][:, 511:512], sets[ns - 1]["HL"][1]
                S.op("dve", lambda e: e.tensor_tensor_scan(out=HL, data0=R1, data1=I1, initial=init, op0=ALU.mult, op1=ALU.add),
                     reads=R1r + I1r + initr, writes=HLr)
                if ns == NS - 1:
                    S.op("dve", lambda e: e.tensor_copy(out=lstate[:, ch:ch + 1], in_=HL[:, 511:512]), reads=HLr, writes=[("lstate", ch)])
                if not state_only:
                    S.op("dve", lambda e: e.scalar_tensor_tensor(out=GL, in0=GL, scalar=1.0, in1=Gv, op0=ALU.add, op1=ALU.mult),
                         reads=GLr + Gr, writes=GLr)
                    S.op("dve", lambda e: e.scalar_tensor_tensor(out=YL[:, ch, ns * 512:(ns + 1) * 512], in0=HL, scalar=0.5, in1=GL,
                                                                 op0=ALU.mult, op1=ALU.mult),
                         reads=HLr + GLr, writes=ares(o_YL, ch, ns * 512, (ns + 1) * 512))

            pend["dve2"] = pend["dve1"]
            pend["dve1"] = back_dve
            pend["q"] = q_dve
            pend["p2"] = part2
        if pend["q"] is not None:
            pend["q"]()
        if pend["p2"] is not None:
            pend["p2"]()
        for key in ("dve2", "dve1"):
            if pend[key] is not None:
                pend[key](0)
                pend[key](1)
        st_ps["n"] = 8

    def kv_halo(l):
        VTl, VTlr = lru_sets[0]["UCB"]
        for g in range(NKV):
            w, wres = wload(w_in[l, c.K0 + g, :, :], KC)
            pb, pres = pbank()
            mm_acc(pb[:, 0:128], pres, [(w[:, k, :], H[:, k, T - 128:T], wres + ares(o_H, k, T - 128, T)) for k in range(KC)])
            act(carry_k[:, g, :], pb[:, 0:128], AF.Identity, pres + VR(l), [("carry_k", g)], bias=vcol(l, "b_in", c.K0 + g))
            w, wres = wload(w_in[l, c.V0 + g, :, :], KC)
            pb, pres = pbank()
            mm_acc(pb[:, 0:128], pres, [(w[:, k, :], H[:, k, T - 128:T], wres + ares(o_H, k, T - 128, T)) for k in range(KC)])
            act(VTl[:, 0:128], pb[:, 0:128], AF.Identity, pres + VR(l), VTlr, bias=vcol(l, "b_in", c.V0 + g))
            pb2, pres2 = pbank()
            pbb = pb2.bitcast(BF16)
            S.op("pe", lambda e, pbb=pbb, VTl=VTl: e.transpose(out=pbb[:, 0:128], in_=VTl[:, 0:128], identity=ident_bf[:]),
                 reads=VTlr + [("ident",)], writes=pres2)
            S.op("dve", lambda e, pbb=pbb, g=g: e.tensor_copy(out=carry_v[:, g, :], in_=pbb[:, 0:128]), reads=pres2, writes=[("carry_v", g)])

    def prepass_tile(l, tt):
        load_norm1(l, tt)
        lru_branch(l, True)
        if tt == NT - 1:
            kv_halo(l)

    def exchange_send(l):
        XS, XSr = sview(0, WX, F32)
        XR, XRr = sview(2 * WX, WX, F32)
        o1, o2, o3 = KC, 5 * KC, 5 * KC + NKV * 128
        S.op("dve", lambda e: e.tensor_copy(out=XS[:, 0:o1], in_=lstate[:]), reads=[("lstate", i) for i in range(KC)], writes=XSr)
        S.op("dve", lambda e: e.tensor_copy(out=XS[:, o1:o2], in_=carry_u[:].rearrange("p k f -> p (k f)")),
             reads=[("carry_u", i) for i in range(KC)], writes=XSr)
        S.op("dve", lambda e: e.tensor_copy(out=XS[:, o2:o3], in_=carry_k[:].rearrange("p g d -> p (g d)")),
             reads=[("carry_k", i) for i in range(NKV)], writes=XSr)
        S.op("dve", lambda e: e.tensor_copy(out=XS[:, o3:WX], in_=carry_v[:].rearrange("p g d -> p (g d)")),
             reads=[("carry_v", i) for i in range(NKV)], writes=XSr)
        S.dma("sp", "st_snd", lambda e: e.dma_start(out=snd[l][:, :], in_=XS), reads=XSr, writes=[("snd", l)])
        groups = [[2 * i, 2 * i + 1] for i in range(c.NCORES // 2)]
        S.dma("pool", "cc", lambda e: e.collective_compute("AllGather", ALU.bypass, replica_groups=groups,
                                                           ins=[snd[l].opt()], outs=[rcv[l].opt()]),
              reads=[("snd", l)], writes=[("rcv", l)], inc=1)

    def exchange_recv(l):
        XR, XRr = sview(2 * WX, WX, F32)
        o1, o2, o3 = KC, 5 * KC, 5 * KC + NKV * 128
        S.dma("sp", "ld_rcv", lambda e: e.dma_start(out=XR, in_=rcv[l][0:128, :]), reads=[("rcv", l)], writes=XRr)
        fl = flag[:, 0:1]
        S.op("dve", lambda e: e.tensor_scalar(out=lstate[:], in0=XR[:, 0:o1], scalar1=fl, scalar2=None, op0=ALU.mult),
             reads=XRr + [("flag",)], writes=[("lstate", i) for i in range(KC)])
        S.op("dve", lambda e: e.tensor_scalar(out=carry_u[:].rearrange("p k f -> p (k f)"), in0=XR[:, o1:o2], scalar1=fl, scalar2=None, op0=ALU.mult),
             reads=XRr + [("flag",)], writes=[("carry_u", i) for i in range(KC)])
        S.op("dve", lambda e: e.tensor_scalar(out=carry_k[:].rearrange("p g d -> p (g d)"), in0=XR[:, o2:o3], scalar1=fl, scalar2=None, op0=ALU.mult),
             reads=XRr + [("flag",)], writes=[("carry_k", i) for i in range(NKV)])
        S.op("dve", lambda e: e.tensor_scalar(out=carry_v[:].rearrange("p g d -> p (g d)"), in0=XR[:, o3:WX], scalar1=fl, scalar2=None, op0=ALU.mult),
             reads=XRr + [("flag",)], writes=[("carry_v", i) for i in range(NKV)])

    def layer_tile(l, tt, last_layer):
        tok0 = tt * T
        xsrc = xin if l == 0 else xs
        xsrc_name = "xin" if l == 0 else "xs"
        load_norm1(l, tt)
        if tt == 0 and c.SPLIT == 2:
            exchange_recv(l)
        lru_branch(l, False)

        sc = 1.0 / math.sqrt(128.0)
        aunit = 0
        for g in range(NKV):
            for j in range(G):
                m = c.Q0 + g * G + j
                w, wres = wload(w_in[l, m, :, :], KC)
                for ns in range(NS):
                    pb, pres = proj(w, wres, H, o_H, KC, ns)
                    act(QG[:, j, ns * 512:(ns + 1) * 512], pb, AF.Identity, pres + VR(l), qgr(j, ns * 512, (ns + 1) * 512), bias=vcol(l, "b_in", m))
            S.op("dve", lambda e, g=g: e.tensor_copy(out=KT[:, 0:128], in_=carry_k[:, g, :]), reads=[("carry_k", g)], writes=ktr(0, 128))
            w, wres = wload(w_in[l, c.K0 + g, :, :], KC)
            for ns in range(NS):
                pb, pres = proj(w, wres, H, o_H, KC, ns)
                act(KT[:, 128 + ns * 512:128 + (ns + 1) * 512], pb, AF.Identity, pres + VR(l), ktr(128 + ns * 512, 128 + (ns + 1) * 512), bias=vcol(l, "b_in", c.K0 + g))
            S.op("dve", lambda e, g=g: e.tensor_copy(out=carry_k[:, g, :], in_=KT[:, T:T + 128]), reads=ktr(T, T + 128), writes=[("carry_k", g)])
            w, wres = wload(w_in[l, c.V0 + g, :, :], KC)
            for ns in range(NS):
                pb, pres = proj(w, wres, H, o_H, KC, ns)
                act(VT[:, ns * 512:(ns + 1) * 512], pb, AF.Identity, pres + VR(l), vtr(ns * 512, (ns + 1) * 512), bias=vcol(l, "b_in", c.V0 + g))
            S.op("dve", lambda e, g=g: e.tensor_copy(out=VTOK[:, 0, :], in_=carry_v[:, g, :]), reads=[("carry_v", g)], writes=vkr(0, 1))
            for q4 in range(NB // 4):
                pb, pres = pbank()
                pbb = pb.bitcast(BF16)
                for i in range(4):
                    b = q4 * 4 + i
                    S.op("pe", lambda e, pbb=pbb, i=i, b=b: e.transpose(out=pbb[:, i * 128:(i + 1) * 128], in_=VT[:, b * 128:(b + 1) * 128], identity=ident_bf[:]),
                         reads=vtr(b * 128, (b + 1) * 128) + [("ident",)], writes=pres, inc=(i == 3))
                S.op("dve", lambda e, pbb=pbb, q4=q4: e.tensor_copy(out=VTOK[:, 1 + q4 * 4:1 + q4 * 4 + 4, :],
                                                                   in_=pbb[:, 0:512].rearrange("p (b d) -> p b d", d=128)),
                     reads=pres, writes=vkr(1 + q4 * 4, 5 + q4 * 4))
            S.op("dve", lambda e, g=g: e.tensor_copy(out=carry_v[:, g, :], in_=VTOK[:, NB, :]), reads=vkr(NB, NB + 1), writes=[("carry_v", g)])
            if last_layer and tt > 0:
                if g == 0:
                    final_norm_sub(tt - 1, 0)
                if g == NKV - 1:
                    final_norm_sub(tt - 1, 1)
            for b in range(NB):
                first = (tt == 0 and b == 0 and c.SPLIT == 1)
                mp = mprev0 if (tt == 0 and b == 0) else mprev
                st = att_sets[aunit % 2]
                aunit += 1
                ERP, ERPr = st["ERP"]; ERC, ERCr = st["ERC"]; EP, EPr = st["EP"]; EC, ECr = st["EC"]; DEN, DENr = st["DEN"]
                qrhs = QG[:, :, b * 128:(b + 1) * 128]
                qres = [r for j in range(G) for r in qgr(j, b * 128, (b + 1) * 128)]
                if not first:
                    pp, ppres = pbank()
                    S.op("pe", lambda e, pp=pp, b=b, qrhs=qrhs: e.matmul(pp.rearrange("p (g q) -> p g q", g=G), lhsT=KT[:, b * 128:(b + 1) * 128], rhs=qrhs, start=True, stop=True),
                         reads=qres + ktr(b * 128, (b + 1) * 128), writes=ppres)
                pc, pcres = pbank()
                S.op("pe", lambda e, pc=pc, b=b, qrhs=qrhs: e.matmul(pc.rearrange("p (g q) -> p g q", g=G), lhsT=KT[:, 128 + b * 128:128 + (b + 1) * 128], rhs=qrhs, start=True, stop=True),
                     reads=qres + ktr(128 + b * 128, 128 + (b + 1) * 128), writes=pcres)
                if not first:
                    act(ERP, pp, AF.Exp, ppres, ERPr, scale=sc)
                    S.op("dve", lambda e, EP=EP, ERP=ERP, mp=mp: e.tensor_tensor(out=EP, in0=ERP, in1=mp[:], op=ALU.mult), reads=ERPr + [("mprev",), ("mprev0",)], writes=EPr)
                act(ERC, pc, AF.Exp, pcres, ERCr, scale=sc)
                S.op("dve", lambda e, EC=EC, ERC=ERC: e.tensor_tensor(out=EC, in0=ERC, in1=mcur[:], op=ALU.mult), reads=ERCr + [("mcur",)], writes=ECr)
                pd, pdres = pbank()
                po, pores = pbank()
                if not first:
                    S.op("pe", lambda e, pd=pd, EP=EP: e.matmul(pd, lhsT=ones_bf[:], rhs=EP, start=True, stop=False), reads=EPr + [("ones",)], writes=pdres, inc=False)
                S.op("pe", lambda e, pd=pd, EC=EC, first=first: e.matmul(pd, lhsT=ones_bf[:], rhs=EC, start=first, stop=True), reads=ECr + [("ones",)], writes=pdres)
                if not first:
                    S.op("pe", lambda e, po=po, EP=EP, b=b: e.matmul(po, lhsT=VTOK[:, b, :], rhs=EP, start=True, stop=False), reads=EPr + vkr(b, b + 1), writes=pores, inc=False)
                S.op("pe", lambda e, po=po, EC=EC, b=b, first=first: e.matmul(po, lhsT=VTOK[:, b + 1, :], rhs=EC, start=first, stop=True), reads=ECr + vkr(b + 1, b + 2), writes=pores)
                for j in range(G):
                    S.op("dve", lambda e, DEN=DEN, pd=pd, j=j, g=g: e.tensor_scalar(out=DEN[:, j * 128:(j + 1) * 128], in0=pd[:, j * 128:(j + 1) * 128],
                                                                                 scalar1=dcol(l, "esk", g * G + j), scalar2=None, op0=ALU.add),
                         reads=pdres + DR(l, "esk"), writes=DENr)
                act(DEN, DEN, AF.Ln, DENr, DENr)
                act(DEN, DEN, AF.Exp, DENr, DENr, scale=-1.0)
                S.op("dve", lambda e, DEN=DEN, po=po, g=g, b=b: e.tensor_tensor(out=YA[:, g * G:(g + 1) * G, b * 128:(b + 1) * 128],
                                                                              in0=po.rearrange("p (g q) -> p g q", g=G),
                                                                              in1=DEN.rearrange("p (g q) -> p g q", g=G), op=ALU.mult),
                     reads=pores + DENr, writes=[r for j in range(G) for r in ares(o_YA, g * G + j, b * 128, (b + 1) * 128)])

        SA, SAr = lru_sets[0]["G"]; SB, SBr = lru_sets[0]["UC"]
        SA2, SA2r = lru_sets[1]["G"]; SB2, SB2r = lru_sets[1]["UC"]
        mu = 0
        for m in range(KC):
            ada_tick(1)
            wA, wAres = wload(w_in[l, c.GA0 + m, :, :], KC)
            wB, wBres = wload(w_in[l, c.GB0 + m, :, :], KC)
            wl, wlres = wload(w_lo[l, m, :, :], KC)
            wt, wtres = wload(w_ao[l, m, :, :], AWC)
            for ns in range(NS):
                sa, sar, sbb, sbr = (SA, SAr, SB, SBr) if mu % 2 == 0 else (SA2, SA2r, SB2, SB2r)
                mu += 1
                pA, pAres = proj(wA, wAres, H, o_H, KC, ns)
                pB, pBres = proj(wB, wBres, H, o_H, KC, ns)
                pL, pLres = proj(wl, wlres, YL, o_YL, KC, ns)
                pT, pTres = proj(wt, wtres, YA, o_YA, AWC, ns)
                act(sa, pA, AF.Sigmoid, pAres + VR(l), sar, bias=vcol(l, "b_in", c.GA0 + m))
                act(sbb, pB, AF.Sigmoid, pBres + VR(l), sbr, bias=vcol(l, "b_in", c.GB0 + m))
                S.op("dve", lambda e, sa=sa, pL=pL: e.tensor_tensor(out=sa, in0=sa, in1=pL, op=ALU.mult), reads=sar + pLres, writes=sar)
                S.op("dve", lambda e, sbb=sbb, pT=pT: e.tensor_tensor(out=sbb, in0=sbb, in1=pT, op=ALU.mult), reads=sbr + pTres, writes=sbr)
                S.op("dve", lambda e, sa=sa, sbb=sbb, m=m, ns=ns: e.tensor_tensor(out=MG[:, m, ns * 512:(ns + 1) * 512], in0=sa, in1=sbb, op=ALU.add),
                     reads=sar + sbr, writes=ares(o_MG, m, ns * 512, (ns + 1) * 512))

        ada_need(l, 2)
        for m in range(KC):
            wo, wores = wload(w_o[l, m, :, :], KC)
            for ns in range(NS):
                xi = (m * NS + ns) % 2
                xcb = XCB[xi]
                S.dma("sp", "ld_xc%d" % xi, lambda e, xcb=xcb, m=m, ns=ns: e.dma_start(out=xcb[:], in_=xsrc[:, m, tok0 + ns * 512:tok0 + (ns + 1) * 512]),
                      reads=[(xsrc_name, m, tt, ns)], writes=[("xcb", xi)])
                pO, pOres = proj(wo, wores, MG, o_MG, KC, ns)
                S.op("dve", lambda e, pO=pO, m=m, ns=ns, xcb=xcb: e.scalar_tensor_tensor(
                    out=X2[:, m, ns * 512:(ns + 1) * 512], in0=pO, scalar=dcol(l, "mod", 2 * KC + m), in1=xcb[:],
                    op0=ALU.mult, op1=ALU.add),
                     reads=pOres + DRm(l, 2) + [("xcb", xi)], writes=ares(o_H, m, ns * 512, (ns + 1) * 512, F32))
            S.dma("sp", "st_x%d" % (m % 4), lambda e, m=m: e.dma_start(out=xs[:, m, tok0:tok0 + T], in_=X2[:, m, :]),
                  reads=ares(o_H, m, 0, T, F32), writes=[("xs", m, tt, n_) for n_ in range(NS)])

        ada_need(l, 4)
        norm_mod(l, X2, o_H, H2, o_YA, lambda kc: dcol(l, "s2", kc), lambda kc: dcol(l, "mod", 3 * KC + kc), 3)
        SG, SGr = lru_sets[0]["G"]; SG2, SG2r = lru_sets[1]["G"]
        fu = 0
        for j in range(FC):
            ada_tick(1)
            wg, wgres = wload(w_f1[l, j, :, :], KC)
            wu, wures = wload(w_f1[l, FC + j, :, :], KC)
            for ns in range(NS):
                sg, sgr = (SG, SGr) if fu % 2 == 0 else (SG2, SG2r)
                fu += 1
                pg, pgres = proj(wg, wgres, H2, o_YA, KC, ns)
                pu, pures = proj(wu, wures, H2, o_YA, KC, ns)
                act(sg, pg, AF.Silu, pgres, sgr)
                S.op("dve", lambda e, sg=sg, pu=pu, j=j, ns=ns: e.tensor_tensor(out=HID[:, j, ns * 512:(ns + 1) * 512], in0=sg, in1=pu, op=ALU.mult),
                     reads=sgr + pures, writes=ares(o_H, j, ns * 512, (ns + 1) * 512))
        ada_need(l, 5)
        pieces = []
        k0 = 0
        while k0 < FC:
            pieces.append((k0, min(16, FC - k0)))
            k0 += 16
        XO = [lru_sets[0]["R1"], lru_sets[0]["A2"], lru_sets[1]["R1"], lru_sets[1]["A2"]]
        xo_i = 0
        for m in range(KC):
            ws = []
            for (k0, kn) in pieces:
                ws.append(wload(w_f2[l, m, :, k0 * 128:(k0 + kn) * 128], kn))
            for ns in range(NS):
                t0, t1 = ns * 512, (ns + 1) * 512
                xi = (m * NS + ns) % 2
                xcb = XCB[xi]
                S.dma("sp", "ld_xc%d" % xi, lambda e, xcb=xcb, m=m, t0=t0, t1=t1: e.dma_start(out=xcb[:], in_=xs[:, m, tok0 + t0:tok0 + t1]),
                      reads=[("xs", m, tt, ns)], writes=[("xcb", xi)])
                pb, pres = pbank()
                pairs = []
                for (k0, kn), (w, wres) in zip(pieces, ws):
                    for k in range(kn):
                        pairs.append((w[:, k, :], HID[:, k0 + k, t0:t1], wres + ares(o_H, k0 + k, t0, t1)))
                mm_acc(pb, pres, pairs)
                xo, xor_ = XO[xo_i % 4]
                xo_i += 1
                S.op("dve", lambda e, pb=pb, m=m, xo=xo, xcb=xcb: e.scalar_tensor_tensor(
                    out=xo, in0=pb, scalar=dcol(l, "mod", 5 * KC + m), in1=xcb[:], op0=ALU.mult, op1=ALU.add),
                     reads=pres + DRm(l, 5) + [("xcb", xi)], writes=xor_)
                S.dma("sp", "st_x%d" % (xo_i % 4), lambda e, xo=xo, m=m, t0=t0, t1=t1: e.dma_start(out=xs[:, m, tok0 + t0:tok0 + t1], in_=xo),
                      reads=xor_, writes=[("xs", m, tt, ns)])

        if last_layer and tt == NT - 1:
            for ns in range(NS):
                final_norm_sub(tt, ns)

    XF = arena[:, o_MG:o_MG + KC * T].bitcast(F32).rearrange("p (c t) -> p c t", t=512)

    def xfres(kc):
        return blk_res("A", o_MG * 2 + kc * 2048, o_MG * 2 + (kc + 1) * 2048)

    def final_norm_sub(tt, ns):
        tok = tt * T + ns * 512
        for kc in range(KC):
            S.dma("sp", "ld_xf%d" % (kc % 4), lambda e, kc=kc: e.dma_start(out=XF[:, kc, :], in_=xs[:, kc, tok:tok + 512]),
                  reads=[("xs", kc, tt, ns)], writes=xfres(kc))
        pb, pres = pbank()
        for kc in range(KC):
            i = kc % 2
            sq = DG[i][:].rearrange("p a b -> p (a b)")
            sqr = [("dg", i, k) for k in range(4)]
            act(sq, XF[:, kc, :], AF.Square, xfres(kc), sqr)
            S.op("pe", lambda e, sq=sq, kc=kc, pb=pb: e.matmul(pb, lhsT=ones_bf[:], rhs=sq, start=(kc == 0), stop=(kc == KC - 1)),
                 reads=sqr + [("ones",)], writes=pres, inc=True)
        RSx, RIx = XCB[0][:], XCB[1][:]
        act(RSx, pb, AF.Sqrt, pres + [("eps",)], [("xcb", 0)], bias=EPS_AP[:, 0:1], scale=1.0 / D)
        S.op("dve", lambda e: e.reciprocal(out=RIx, in_=RSx), reads=[("xcb", 0)], writes=[("xcb", 1)])
        for kc in range(KC):
            S.op("dve", lambda e, kc=kc: e.scalar_tensor_tensor(out=XF[:, kc, :], in0=XF[:, kc, :], scalar=vcol(0, "fg", kc), in1=RIx,
                                                                op0=ALU.mult, op1=ALU.mult),
                 reads=xfres(kc) + [("xcb", 1)] + VR(0), writes=xfres(kc))
            S.dma("sp", "st_y%d" % (kc % 4), lambda e, kc=kc: e.dma_start(out=yout[:, kc, tok:tok + 512], in_=XF[:, kc, :]),
                  reads=xfres(kc), writes=[("y", kc, tt, ns)])

    ONE_AP = sb("one_ap", [128, 1], F32)
    S.op("dve", lambda e: e.memset(ONE_AP[:], 1.0), writes=[("one",)])

    for l in range(DEPTH):
        layer_setup(l)
    for l in range(DEPTH):
        layer_state_reset(l)
        if c.SPLIT == 2:
            for tt in range(NT):
                prepass_tile(l, tt)
            exchange_send(l)
        for tt in range(NT):
            layer_tile(l, tt, l == DEPTH - 1)
    S.final_wait("sp", ["st_y%d" % i for i in range(4)] + ["st_x%d" % i for i in range(4)])
    S.emit()
    es.close()
    return nc


def _tile_w(W, kcn=None):
    K, N = W.shape
    kc, mc = K // 128, N // 128
    return np.ascontiguousarray(W.reshape(kc, 128, mc, 128).transpose(2, 1, 0, 3).reshape(mc, 128, kc * 128))


def _pcol(v):
    return np.ascontiguousarray(v.reshape(-1, 128).T)


def prep_shared(cfg, inp):
    c = cfg
    DEPTH, KC = c.DEPTH, c.KC
    f = lambda a: np.asarray(a, dtype=np.float32)
    sh = {}
    sh["ada_w"] = np.stack([_tile_w(f(inp["ada_w"][l])) for l in range(DEPTH)])
    sh["w_in"] = np.stack([_tile_w(f(inp["w_in"][l])) for l in range(DEPTH)])
    sh["lru_wa"] = np.stack([np.ascontiguousarray(f(inp["lru_wa"][l]).transpose(1, 0, 2).reshape(128, KC * 128)) for l in range(DEPTH)])
    sh["lru_wx"] = np.stack([np.ascontiguousarray(f(inp["lru_wx"][l]).transpose(1, 0, 2).reshape(128, KC * 128)) for l in range(DEPTH)])
    sh["w_lo"] = np.stack([_tile_w(f(inp["w_lru_out"][l])) for l in range(DEPTH)])
    sh["w_ao"] = np.stack([_tile_w(f(inp["w_attn_out"][l])) for l in range(DEPTH)])
    sh["w_o"] = np.stack([_tile_w(f(inp["w_o"][l])) for l in range(DEPTH)])
    sh["w_f1"] = np.stack([_tile_w(f(inp["w_ffn_in"][l])) for l in range(DEPTH)])
    sh["w_f2"] = np.stack([_tile_w(f(inp["w_ffn_out"][l])) for l in range(DEPTH)])
    vecs = np.zeros((DEPTH, 128, c.NV), np.float32)
    for l in range(DEPTH):
        def put(nm, arr):
            o = c.VO[nm]
            vecs[l, :, o:o + arr.shape[1]] = arr
        put("ada_b", _pcol(f(inp["ada_b"][l])))
        put("n1g", _pcol(f(inp["norm1_g"][l])))
        put("n2g", _pcol(f(inp["norm2_g"][l])))
        put("b_in", _pcol(f(inp["b_in"][l])))
        cw = f(inp["conv_w"][l])
        put("cw", np.concatenate([_pcol(cw[k]) for k in range(4)], axis=1))
        put("cb", _pcol(f(inp["conv_b"][l])))
        put("ba", _pcol(f(inp["lru_ba"][l])))
        put("bx", _pcol(f(inp["lru_bx"][l])))
        put("lam", _pcol(f(inp["lru_lambda"][l])))
        put("sink", np.broadcast_to(f(inp["sinks"][l])[None, :], (128, c.NQ)))
        put("fg", _pcol(f(inp["final_g"])))
    sh["vecs"] = vecs
    return sh


def prep_core(cfg, inp, r):
    c = cfg
    b, half = r // c.SPLIT, r % c.SPLIT
    x = np.asarray(inp["x"][b], dtype=np.float32)[half * c.TOK:(half + 1) * c.TOK]
    xT = np.ascontiguousarray(x.T.reshape(c.KC, 128, c.TOK).transpose(1, 0, 2))
    cv = _pcol(np.asarray(inp["c"][b], dtype=np.float32))
    fl = np.full((128, 1), float(half), np.float32)
    return {"xin": xT, "cvec": cv, "flag": fl}


_NC_CACHE = {}


def run_cfg(cfg, inp):
    key = (cfg.D, cfg.NQ, cfg.NKV, cfg.FH, cfg.SEQ, cfg.DEPTH, cfg.T, cfg.SPLIT, cfg.BATCH)
    if key not in _NC_CACHE:
        _NC_CACHE[key] = build_program(cfg)
    nc = _NC_CACHE[key]
    sh = prep_shared(cfg, inp)
    in_maps = []
    for r in range(cfg.NCORES):
        m = dict(sh)
        m.update(prep_core(cfg, inp, r))
        in_maps.append(m)
    res = run_bass_kernel_spmd(nc, in_maps, core_ids=list(range(cfg.NCORES)))
    out = np.empty((cfg.BATCH, cfg.SEQ, cfg.D), np.float32)
    for r in range(cfg.NCORES):
        b, half = r // cfg.SPLIT, r % cfg.SPLIT
        y = np.asarray(res.results[r]["y"])
        out[b, half * cfg.TOK:(half + 1) * cfg.TOK, :] = y.transpose(1, 0, 2).reshape(cfg.D, cfg.TOK).T
    return out


def kernel(**inputs):
    cfg = Cfg()
    return run_cfg(cfg, inputs)
```

```python
import math
from contextlib import ExitStack

import numpy as np
import concourse.bass as bass
import concourse.mybir as mybir
from concourse.bass_utils import run_bass_kernel_spmd

F32 = mybir.dt.float32
BF16 = mybir.dt.bfloat16
AF = mybir.ActivationFunctionType
ALU = mybir.AluOpType

EPS = 1e-6
LRU_C = 8.0


class Cfg:
    def __init__(self, D=2048, NQ=16, NKV=4, FH=5632, SEQ=4096, BATCH=4, DEPTH=2, T=1024, SPLIT=2):
        self.D, self.NQ, self.NKV, self.FH, self.SEQ, self.BATCH, self.DEPTH, self.T = D, NQ, NKV, FH, SEQ, BATCH, DEPTH, T
        self.SPLIT = SPLIT
        self.NCORES = BATCH * SPLIT
        self.KC = D // 128
        self.G = NQ // NKV
        self.AWC = NQ
        self.FC = FH // 128
        self.TOK = SEQ // SPLIT
        self.NT = self.TOK // T
        self.NS = T // 512
        self.NB = T // 128
        self.CIN = 2 * D + NQ * 128 + 2 * NKV * 128 + 2 * D
        self.MC = self.CIN // 128
        KC = self.KC
        self.U0, self.GT0, self.Q0 = 0, KC, 2 * KC
        self.K0 = self.Q0 + NQ
        self.V0 = self.K0 + NKV
        self.GA0 = self.V0 + NKV
        self.GB0 = self.GA0 + KC
        o = {}
        p = 0
        for name, n in [("ada_b", 6 * KC), ("n1g", KC), ("n2g", KC), ("b_in", self.MC), ("cw", 4 * KC), ("cb", KC),
                        ("ba", KC), ("bx", KC), ("lam", KC), ("sink", NQ), ("fg", KC)]:
            o[name] = p
            p += n
        self.VO, self.NV = o, p
        d = {}
        p = 0
        for name, n in [("mod", 6 * KC), ("s1", KC), ("s2", KC), ("nc8", KC), ("nc16", KC), ("hnc8", KC), ("hba", KC), ("hbx", KC), ("z", KC), ("t", KC), ("esk", NQ)]:
            d[name] = p
            p += n
        self.DO, self.NDV = d, p


class Sch:
    def __init__(self, nc, es):
        self.nc = nc
        self.es = es
        self.prog = {k: [] for k in ("pe", "act", "dve", "pool", "sp")}
        self.semh = {}
        self.cnt = {}
        for k in ("pe", "act", "dve", "pool"):
            self.semh[k] = es.enter_context(nc.semaphore("s_" + k))
            self.cnt[k] = 0
        self.clock = {k: {} for k in self.prog}
        self.snap = {}
        self.lastw = {}
        self.readers = {}
        self.nwait = 0

    def slot(self, name):
        if name not in self.semh:
            self.semh[name] = self.es.enter_context(self.nc.semaphore("d_" + name))
            self.cnt[name] = 0
        return name

    def _deps(self, eng, reads, writes):
        deps = {}

        def add(k):
            if k is None:
                return
            f, n = k
            if deps.get(f, 0) < n:
                deps[f] = n

        for r in reads:
            add(self.lastw.get(r))
            if r[0] == "ps":
                for f, n in self.readers.get(r, {}).items():
                    if f != eng:
                        add((f, n))
        for w in writes:
            add(self.lastw.get(w))
            for f, n in self.readers.get(w, {}).items():
                if f == eng and eng == "pe":
                    continue
                add((f, n))
        return deps

    def _emit_waits(self, eng, deps):
        ck = self.clock[eng]
        for f, n in deps.items():
            if f == eng and eng == "pe":
                continue
            if ck.get(f, 0) >= n:
                continue
            assert n <= self.cnt[f], ("dependency on unmaterialised count", eng, f, n, self.cnt[f])
            h = self.semh[f]
            self.prog[eng].append(("w", h, n))
            self.nwait += 1
            sn = self.snap.get((f, n))
            if sn:
                for a, b in sn.items():
                    if ck.get(a, 0) < b:
                        ck[a] = b
            ck[f] = max(ck.get(f, 0), n)

    def _record(self, key, reads, writes):
        for r in reads:
            self.readers.setdefault(r, {})[key[0]] = key[1]
        for w in writes:
            self.lastw[w] = key
            self.readers[w] = {}

    def op(self, eng, fn, reads=(), writes=(), inc=True):
        deps = self._deps(eng, reads, writes)
        self._emit_waits(eng, deps)
        n = self.cnt[eng] + 1
        if inc:
            self.cnt[eng] = n
            self.prog[eng].append(("i", fn, self.semh[eng], 1))
            sn = dict(self.clock[eng])
            sn[eng] = n
            self.snap[(eng, n)] = sn
            self.clock[eng][eng] = max(self.clock[eng].get(eng, 0), 0)
        else:
            self.prog[eng].append(("i", fn, None, 0))
        self._record((eng, n), reads, writes)

    def dma(self, q, slot, fn, reads=(), writes=(), inc=16):
        self.slot(slot)
        deps = self._deps(q, reads, writes)
        if self.cnt[slot] > 0:
            deps[slot] = max(deps.get(slot, 0), self.cnt[slot])
        self._emit_waits(q, deps)
        n = self.cnt[slot] + inc
        self.cnt[slot] = n
        self.prog[q].append(("i", fn, self.semh[slot], inc))
        self.snap[(slot, n)] = dict(self.clock[q])
        self._record((slot, n), reads, writes)

    def final_wait(self, eng, slots):
        for s in slots:
            if self.cnt.get(s, 0) > 0:
                self.prog[eng].append(("w", self.semh[s], self.cnt[s]))

    def emit(self):
        nc = self.nc
        prog = self.prog

        def run(e, lst):
            for it in lst:
                if it[0] == "w":
                    e.wait_ge(it[1], it[2])
                else:
                    ins = it[1](e)
                    if it[2] is not None:
                        ins.then_inc(it[2], it[3])

        with nc.Block() as block:
            @block.tensor
            def _(e):
                run(e, prog["pe"])

            @block.scalar
            def _(e):
                run(e, prog["act"])

            @block.vector
            def _(e):
                run(e, prog["dve"])

            @block.gpsimd
            def _(e):
                run(e, prog["pool"])

            @block.sync
            def _(e):
                run(e, prog["sp"])


def blk_res(name, lo, hi, blk=1024):
    return [(name, b) for b in range(lo // blk, (hi - 1) // blk + 1)]


def build_program(cfg):
    c = cfg
    D, KC, T, NS, NB, NT, TOK, FC, G, NQ, NKV, AWC = c.D, c.KC, c.T, c.NS, c.NB, c.NT, c.TOK, c.FC, c.G, c.NQ, c.NKV, c.AWC
    DEPTH = c.DEPTH
    nc = bass.Bass("TRN2", target_bir_lowering=False)
    es = ExitStack()

    def din(name, shape, dt=F32):
        return nc.dram_tensor(name, list(shape), dt, kind="ExternalInput").ap()

    xin = din("xin", [128, KC, TOK])
    cact_in = din("cvec", [128, KC])
    vecs_in = din("vecs", [DEPTH, 128, c.NV])
    ada_w = din("ada_w", [DEPTH, 6 * KC, 128, KC * 128])
    w_in = din("w_in", [DEPTH, c.MC, 128, KC * 128])
    lru_wa = din("lru_wa", [DEPTH, 128, KC * 128])
    lru_wx = din("lru_wx", [DEPTH, 128, KC * 128])
    w_lo = din("w_lo", [DEPTH, KC, 128, KC * 128])
    w_ao = din("w_ao", [DEPTH, KC, 128, AWC * 128])
    w_o = din("w_o", [DEPTH, KC, 128, KC * 128])
    w_f1 = din("w_f1", [DEPTH, 2 * FC, 128, KC * 128])
    w_f2 = din("w_f2", [DEPTH, KC, 128, FC * 128])
    yout = nc.dram_tensor("y", [128, KC, TOK], F32, kind="ExternalOutput").ap()
    xs = nc.dram_tensor("xs", [128, KC, TOK], F32, kind="Internal").ap()
    flag_in = din("flag", [128, 1])
    snd = [nc.dram_tensor("snd%d" % l, [128, 5 * KC + 2 * NKV * 128], F32).ap() for l in range(DEPTH)]
    rcv = [nc.dram_tensor("rcv%d" % l, [256, 5 * KC + 2 * NKV * 128], F32).ap() for l in range(DEPTH)]

    def sb(name, shape, dt):
        return es.enter_context(nc.sbuf_tensor(name, list(shape), dt))

    S = Sch(nc, es)

    ASZ = (3 * KC + AWC) * T
    arena = sb("arena", [128, ASZ], BF16)
    o_H, o_YL, o_MG, o_YA = 0, KC * T, 2 * KC * T, 3 * KC * T

    def aview(off, nchunk, dt=BF16):
        n = nchunk * T * (2 if dt == F32 else 1)
        v = arena[:, off:off + n]
        if dt == F32:
            v = v.bitcast(F32)
        return v.rearrange("p (c t) -> p c t", t=T)

    def ares(off, ci, t0, t1, dt=BF16):
        esz = 4 if dt == F32 else 2
        lo = off * 2 + (ci * T + t0) * esz
        hi = off * 2 + (ci * T + t1) * esz
        return blk_res("A", lo, hi)

    H = aview(o_H, KC)
    YL = aview(o_YL, KC)
    MG = aview(o_MG, KC)
    YA = aview(o_YA, AWC)
    XT = aview(o_YL, KC, F32)
    X2 = aview(o_H, KC, F32)
    H2 = aview(o_YA, KC)
    HID = aview(o_H, FC)
    assert FC <= 3 * KC and AWC >= KC

    NU = 7 * 1024 + 512
    LRU_SZ = 2 * NU + 2 * (T + 8)
    ATT_SZ = G * T + (128 + T) + T + (NB + 1) * 128 + 4 * 512 * 2 + 2 * 1024 * 2
    SSZ = max(LRU_SZ, ATT_SZ)
    scr = sb("scr", [128, SSZ], BF16)

    def sres(off, n):
        return blk_res("S", off * 2, (off + n) * 2, blk=256)

    def sview(off, n, dt=BF16):
        m = n * (2 if dt == F32 else 1)
        v = scr[:, off:off + m]
        return (v.bitcast(F32) if dt == F32 else v), sres(off, m)

    lru_sets = []
    p = 0
    LNAMES = ("G", "UC", "R1", "A2", "I1", "HL", "GL")
    for u in range(2):
        st = {}
        for nm in LNAMES:
            st[nm] = sview(p, 512, F32)
            p += 1024
        st["UCB"] = sview(p, 512, BF16)
        p += 512
        lru_sets.append(st)
    UB = []
    for u in range(2):
        UB.append(sview(p, T + 8, BF16))
        p += (T + 8)
    if KC * T >= 2 * NU:
        xplaces = [(arena, "A", o_MG, 1024), (arena, "A", o_YA, 1024)]
    else:
        xten = sb("xtra", [128, 4 * NU], BF16)
        xplaces = [(xten, "X", 0, 256), (xten, "X", 2 * NU, 256)]
    for (xten_, xname, xbase, xblk) in xplaces:
        q = xbase
        for u in range(2):
            st = {}
            for nm in LNAMES + ("UCB",):
                n_ = 512 if nm == "UCB" else 1024
                v = xten_[:, q:q + n_]
                st[nm] = ((v if nm == "UCB" else v.bitcast(F32)), blk_res(xname, q * 2, (q + n_) * 2, blk=xblk))
                q += n_
            lru_sets.append(st)
    p = 0
    oQG = p
    QG_ap, QG_res = sview(p, G * T); p += G * T
    QG = QG_ap.rearrange("p (g t) -> p g t", t=T)
    oKT = p
    KT, KT_res = sview(p, 128 + T); p += 128 + T
    oVT = p
    VT, VT_res = sview(p, T); p += T
    oVTOK = p
    VTOK_ap, VTOK_res = sview(p, (NB + 1) * 128); p += (NB + 1) * 128
    qgr = lambda j, t0, t1: sres(oQG + j * T + t0, t1 - t0)
    ktr = lambda a, b: sres(oKT + a, b - a)
    vtr = lambda a, b: sres(oVT + a, b - a)
    vkr = lambda b0, b1: sres(oVTOK + b0 * 128, (b1 - b0) * 128)
    VTOK = VTOK_ap.rearrange("p (b d) -> p b d", d=128)
    att_sets = []
    for u in range(2):
        st = {}
        for nm in ("ERP", "ERC", "EP", "EC"):
            st[nm] = sview(p, 512); p += 512
        att_sets.append(st)
    for u in range(2):
        att_sets[u]["DEN"] = sview(p, 512, F32); p += 1024
    assert p <= SSZ

    vecs = [sb("vecs%d" % l, [128, c.NV], F32) for l in range(DEPTH)]
    dv = [sb("dv%d" % l, [128, c.NDV], F32) for l in range(DEPTH)]
    cact32 = sb("cact32", [128, KC], F32)
    cact = sb("cact", [128, KC], BF16)
    ones_bf = sb("ones_bf", [128, 128], BF16)
    ident_bf = sb("ident_bf", [128, 128], BF16)
    _a, identf_r = sview(0, 128, F32)
    ident_f = _a
    _a, mcurf_r = sview(1024, G * 128, F32)
    mcur_f = _a.rearrange("p (g q) -> p g q", g=G)
    _a, mprevf_r = sview(1024 + 2 * G * 128, G * 128, F32)
    mprev_f = _a.rearrange("p (g q) -> p g q", g=G)
    mcur = sb("mcur", [128, G * 128], BF16)
    mprev0 = sb("mprev0", [128, G * 128], BF16)
    flag = sb("flag_sb", [128, 1], F32)
    WX = 5 * KC + 2 * NKV * 128
    mprev = sb("mprev", [128, G * 128], BF16)
    lstate = sb("lstate", [128, KC], F32)
    carry_u = sb("carry_u", [128, KC, 4], BF16)
    DG = [sb("dg%d" % i, [128, 4, 128], BF16) for i in range(2)]
    carry_k = sb("carry_k", [128, NKV, 128], BF16)
    carry_v = sb("carry_v", [128, NKV, 128], BF16)
    SQ = [lru_sets[i]["UCB"] for i in range(2)]
    TMPN = [lru_sets[i]["UC"] for i in range(2)]
    RS, RSr = lru_sets[0]["R1"]
    RI, RIr = lru_sets[0]["A2"]
    XCB = [sb("xcb%d" % i, [128, 512], F32) for i in range(2)]
    NRING = 6
    ring = [sb("ring%d" % i, [128, 16 * 128], BF16) for i in range(NRING)]
    ps = es.enter_context(nc.psum_tensor("ps", [128, 8, 512], F32))

    st_ring = {"i": 0}
    st_ps = {"i": 0}

    st_ps["n"] = 8

    def pbank(fixed=None):
        if fixed is not None:
            return ps[:, fixed, :], [("ps", fixed)]
        i = st_ps["i"] % st_ps["n"]
        st_ps["i"] += 1
        return ps[:, i, :], [("ps", i)]

    def wload(src2d, kcn):
        i = st_ring["i"] % NRING
        st_ring["i"] += 1
        dst = ring[i][:, 0:kcn * 128]
        res = [("ring", i)]
        S.dma("pool", "ring%d" % i, lambda e, dst=dst, src=src2d: e.dma_start(out=dst, in_=src), reads=(), writes=res)
        return dst.rearrange("p (k m) -> p k m", m=128), res

    def mm_acc(pb, pres, pairs, extra_reads=()):
        n = len(pairs)
        for i, (l, r, rd) in enumerate(pairs):
            S.op("pe", lambda e, l=l, r=r, i=i: e.matmul(pb, lhsT=l, rhs=r, start=(i == 0), stop=(i == n - 1)),
                 reads=list(rd) + list(extra_reads), writes=pres, inc=(i == n - 1))

    def act(out, in_, func, reads, writes, bias=None, scale=None):
        kw = {}
        if bias is not None:
            kw["bias"] = bias
        if scale is not None:
            kw["scale"] = scale
        S.op("act", lambda e: e.activation(out=out, in_=in_, func=func, **kw), reads=reads, writes=writes)

    VR = lambda l: [("vecs", l)]
    DR = lambda l, nm: [("dv", l, nm)]
    S.op("dve", lambda e: e.memset(ones_bf[:], 1.0), writes=[("ones",)])
    S.op("pool", lambda e: e.memset(ident_f, 1.0), writes=identf_r)
    S.op("pool", lambda e: e.affine_select(out=ident_f, in_=ident_f, compare_op=ALU.is_equal, fill=0.0, base=0,
                                           pattern=[[-1, 128]], channel_multiplier=1),
         reads=identf_r, writes=identf_r)
    S.op("pool", lambda e: e.memset(mcur_f, 1.0), writes=mcurf_r)
    S.op("pool", lambda e: e.affine_select(out=mcur_f, in_=mcur_f, compare_op=ALU.is_ge, fill=0.0, base=0,
                                           pattern=[[0, G], [1, 128]], channel_multiplier=-1),
         reads=mcurf_r, writes=mcurf_r)
    S.op("pool", lambda e: e.memset(mprev_f, 1.0), writes=mprevf_r)
    S.op("pool", lambda e: e.affine_select(out=mprev_f, in_=mprev_f, compare_op=ALU.is_gt, fill=0.0, base=0,
                                           pattern=[[0, G], [-1, 128]], channel_multiplier=1),
         reads=mprevf_r, writes=mprevf_r)
    S.op("dve", lambda e: e.tensor_copy(out=ident_bf[:], in_=ident_f), reads=identf_r, writes=[("ident",)])
    S.op("dve", lambda e: e.tensor_copy(out=mcur[:].rearrange("p (g q) -> p g q", g=G), in_=mcur_f), reads=mcurf_r, writes=[("mcur",)])
    S.op("dve", lambda e: e.tensor_copy(out=mprev[:].rearrange("p (g q) -> p g q", g=G), in_=mprev_f), reads=mprevf_r, writes=[("mprev",)])
    S.dma("sp", "ld_c", lambda e: e.dma_start(out=cact32[:], in_=cact_in[:, :]), writes=[("cact32",)])
    S.dma("sp", "ld_f", lambda e: e.dma_start(out=flag[:], in_=flag_in[:, :]), writes=[("flag",)])
    S.op("dve", lambda e: e.tensor_scalar(out=mprev0[:], in0=mprev[:], scalar1=flag[:, 0:1], scalar2=None, op0=ALU.mult),
         reads=[("mprev",), ("flag",)], writes=[("mprev0",)])
    for l in range(DEPTH):
        S.dma("sp", "ld_v%d" % l, lambda e, l=l: e.dma_start(out=vecs[l][:], in_=vecs_in[l, :, :]), writes=VR(l))
    act(cact[:], cact32[:], AF.Silu, [("cact32",)], [("cact",)])

    def vcol(l, nm, i, n=1):
        o = c.VO[nm] + i
        return vecs[l][:, o:o + n]

    def dcol(l, nm, i, n=1):
        o = c.DO[nm] + i
        return dv[l][:, o:o + n]

    ada_q = [(l_, j_) for l_ in range(DEPTH) for j_ in range(6 * KC)]
    ada_state = {"derived": set()}

    def DRm(l, part):
        return [("dv", l, "mod", part)]

    def MODR(l):
        return [("dv", l, "mod", p_) for p_ in range(6)]

    def ada_job(l, j):
        w, wres = wload(ada_w[l, j, :, :], KC)
        pb, pres = pbank()
        for kc in range(KC):
            S.op("pe", lambda e, kc=kc, w=w, pb=pb: e.matmul(pb[:, 0:1], lhsT=w[:, kc, :], rhs=cact[:, kc:kc + 1],
                                                         start=(kc == 0), stop=(kc == KC - 1)),
                 reads=wres + [("cact",)], writes=pres, inc=(kc == KC - 1))
        S.op("dve", lambda e, pb=pb: e.tensor_tensor(out=dcol(l, "mod", j), in0=pb[:, 0:1], in1=vcol(l, "ada_b", j), op=ALU.add),
             reads=pres + VR(l), writes=DRm(l, j // KC))

    def ada_tick(n=1):
        for _ in range(n):
            if ada_q:
                ada_job(*ada_q.pop(0))

    def ada_need(l, part):
        while ada_q and ada_q[0] <= (l, (part + 1) * KC - 1):
            ada_job(*ada_q.pop(0))
        if part >= 1 and (l, "s1") not in ada_state["derived"]:
            ada_state["derived"].add((l, "s1"))
            S.op("dve", lambda e: e.scalar_tensor_tensor(out=dcol(l, "s1", 0, KC), in0=dcol(l, "mod", KC, KC), scalar=1.0,
                                                         in1=vcol(l, "n1g", 0, KC), op0=ALU.add, op1=ALU.mult),
                 reads=DRm(l, 1) + VR(l), writes=DR(l, "s1"))
        if part >= 4 and (l, "s2") not in ada_state["derived"]:
            ada_state["derived"].add((l, "s2"))
            S.op("dve", lambda e: e.scalar_tensor_tensor(out=dcol(l, "s2", 0, KC), in0=dcol(l, "mod", 4 * KC, KC), scalar=1.0,
                                                         in1=vcol(l, "n2g", 0, KC), op0=ALU.add, op1=ALU.mult),
                 reads=DRm(l, 4) + VR(l), writes=DR(l, "s2"))

    def layer_setup(l):
        zc, tc_ = dcol(l, "z", 0, KC), dcol(l, "t", 0, KC)
        act(zc, vcol(l, "lam", 0, KC), AF.Exp, VR(l), DR(l, "z"), scale=-1.0)
        S.op("dve", lambda e: e.tensor_scalar(out=tc_, in0=zc, scalar1=0.2, scalar2=-0.25, op0=ALU.mult, op1=ALU.add),
             reads=DR(l, "z"), writes=DR(l, "t"))
        for cst in (1.0 / 3.0, -0.5, 1.0):
            S.op("dve", lambda e: e.tensor_tensor(out=tc_, in0=tc_, in1=zc, op=ALU.mult), reads=DR(l, "z") + DR(l, "t"), writes=DR(l, "t"))
            S.op("dve", lambda e, cst=cst: e.tensor_scalar(out=tc_, in0=tc_, scalar1=cst, scalar2=None, op0=ALU.add),
                 reads=DR(l, "t"), writes=DR(l, "t"))
        S.op("dve", lambda e: e.scalar_tensor_tensor(out=dcol(l, "nc8", 0, KC), in0=tc_, scalar=-LRU_C, in1=zc, op0=ALU.mult, op1=ALU.mult),
             reads=DR(l, "z") + DR(l, "t"), writes=DR(l, "nc8"))
        S.op("dve", lambda e: e.tensor_scalar(out=dcol(l, "nc16", 0, KC), in0=dcol(l, "nc8", 0, KC), scalar1=2.0, scalar2=None, op0=ALU.mult),
             reads=DR(l, "nc8"), writes=DR(l, "nc16"))
        act(dcol(l, "esk", 0, NQ), vcol(l, "sink", 0, NQ), AF.Exp, VR(l), DR(l, "esk"))
        S.op("dve", lambda e: e.tensor_scalar(out=dcol(l, "hnc8", 0, KC), in0=dcol(l, "nc8", 0, KC), scalar1=0.5, scalar2=None, op0=ALU.mult),
             reads=DR(l, "nc8"), writes=DR(l, "hnc8"))
        S.op("dve", lambda e: e.tensor_scalar(out=dcol(l, "hba", 0, KC), in0=vcol(l, "ba", 0, KC), scalar1=0.5, scalar2=None, op0=ALU.mult),
             reads=VR(l), writes=DR(l, "hba"))
        S.op("dve", lambda e: e.tensor_scalar(out=dcol(l, "hbx", 0, KC), in0=vcol(l, "bx", 0, KC), scalar1=0.5, scalar2=None, op0=ALU.mult),
             reads=VR(l), writes=DR(l, "hbx"))

    def layer_state_reset(l):
        S.op("dve", lambda e: e.memset(lstate[:], 0.0), writes=[("lstate", i) for i in range(KC)])
        S.op("dve", lambda e: e.memset(carry_u[:], 0.0), writes=[("carry_u", i) for i in range(KC)])
        S.op("dve", lambda e: e.memset(carry_k[:], 0.0), writes=[("carry_k", i) for i in range(NKV)])
        S.op("dve", lambda e: e.memset(carry_v[:], 0.0), writes=[("carry_v", i) for i in range(NKV)])

    def norm_stats(XV, x_off, t0, t1):
        pb, pres = pbank()
        for kc in range(KC):
            sq, sqr = SQ[kc % 2]
            xr = ares(x_off, kc, t0, t1, F32)
            act(sq, XV[:, kc, t0:t1], AF.Square, xr, sqr)
            S.op("pe", lambda e, sq=sq, kc=kc: e.matmul(pb, lhsT=ones_bf[:], rhs=sq, start=(kc == 0), stop=(kc == KC - 1)),
                 reads=sqr + [("ones",)], writes=pres, inc=True)
        act(RS, pb, AF.Sqrt, pres + [("eps",)], RSr, bias=EPS_AP[:, 0:1], scale=1.0 / D)
        S.op("dve", lambda e: e.reciprocal(out=RI, in_=RS), reads=RSr, writes=RIr)

    def norm_mod(l, XV, x_off, OUT, out_off, scol, bcol, mpart, need=None):
        for ns in range(NS):
            t0, t1 = ns * 512, (ns + 1) * 512
            norm_stats(XV, x_off, t0, t1)
            if ns == 0 and need is not None:
                need()
            for kc in range(KC):
                tm, tmr = TMPN[kc % 2]
                xr = ares(x_off, kc, t0, t1, F32)
                S.op("dve", lambda e, tm=tm, kc=kc, t0=t0, t1=t1: e.scalar_tensor_tensor(out=tm, in0=XV[:, kc, t0:t1], scalar=scol(kc), in1=RI,
                                                                                     op0=ALU.mult, op1=ALU.mult),
                     reads=xr + RIr + DR(l, "s1") + DR(l, "s2"), writes=tmr)
                act(OUT[:, kc, t0:t1], tm, AF.Identity, tmr + DRm(l, mpart), ares(out_off, kc, t0, t1), bias=bcol(kc))

    EPS_AP = sb("eps_ap", [128, 1], F32)
    S.op("dve", lambda e: e.memset(EPS_AP[:], EPS), writes=[("eps",)])

    def proj(w, wres, RHS, rhs_off, nk, ns):
        t0, t1 = ns * 512, (ns + 1) * 512
        pb, pres = pbank()
        mm_acc(pb, pres, [(w[:, k, :], RHS[:, k, t0:t1], wres + ares(rhs_off, k, t0, t1)) for k in range(nk)])
        return pb, pres

    def load_norm1(l, tt):
        tok0 = tt * T
        xsrc = xin if l == 0 else xs
        xsrc_name = "xin" if l == 0 else "xs"
        for kc in range(KC):
            S.dma("sp", "ld_xt%d" % (kc % 4), lambda e, kc=kc: e.dma_start(out=XT[:, kc, :], in_=xsrc[:, kc, tok0:tok0 + T]),
                  reads=[(xsrc_name, kc, tt, n_) for n_ in range(NS)], writes=ares(o_YL, kc, 0, T, F32))
        norm_mod(l, XT, o_YL, H, o_H, lambda kc: dcol(l, "s1", kc), lambda kc: dcol(l, "mod", 0 * KC + kc), 0,
                 need=lambda: ada_need(l, 1))

    def lru_branch(l, state_only):
        st_ps["n"] = 4
        st_ps["i"] = 0
        GC0 = math.sqrt(0.044715)
        GC1 = 0.7978845608028654
        pend = {"q": None, "p2": None, "dve1": None, "dve2": None}

        def pre(ch):
            cx = {}
            ub, _ = UB[ch % 2]
            cx["ub"], cx["ubname"], cx["dg"] = ub, "ubh%d" % (ch % 2), DG[ch % 2]
            dg = cx["dg"]
            S.op("dve", lambda e, ub=ub, ch=ch: e.tensor_copy(out=ub[:, 1:4], in_=carry_u[:, ch, 0:3]),
                 reads=[("carry_u", ch)], writes=[(cx["ubname"],)])
            for k in range(4):
                S.op("dve", lambda e, dg=dg, k=k, ch=ch: e.tensor_scalar(out=dg[:, k, :], in0=ident_bf[:], scalar1=vcol(l, "cw", k * KC + ch),
                                                                       scalar2=None, op0=ALU.mult),
                     reads=[("ident",)] + VR(l), writes=[("dg", ch % 2, k)])
            cx["ubr"] = [[("ub", ch % 2, ns)] for ns in range(NS)]
            return cx

        def pre_b(ch, cx):
            ada_tick(2)
            if not state_only:
                cx["wg"] = wload(w_in[l, c.GT0 + ch, :, :], KC)
            cx["wa"] = wload(lru_wa[l, :, ch * 128:(ch + 1) * 128], 1)
            cx["wx"] = wload(lru_wx[l, :, ch * 128:(ch + 1) * 128], 1)

        def uproj(ch, ns, cx):
            ub = cx["ub"]
            if "wu" not in cx:
                cx["wu"] = wload(w_in[l, c.U0 + ch, :, :], KC)
            wu, wures = cx["wu"]
            pb, pres = proj(wu, wures, H, o_H, KC, ns)
            S.op("dve", lambda e, ub=ub, pb=pb, ns=ns, ch=ch: e.tensor_scalar(out=ub[:, 4 + ns * 512:4 + (ns + 1) * 512], in0=pb,
                                                                             scalar1=vcol(l, "b_in", c.U0 + ch), scalar2=None, op0=ALU.add),
                 reads=pres + VR(l), writes=cx["ubr"][ns])
            if ns == NS - 1:
                S.op("dve", lambda e, ub=ub, ch=ch: e.tensor_copy(out=carry_u[:, ch, 0:3], in_=ub[:, T + 1:T + 4]),
                     reads=cx["ubr"][NS - 1], writes=[("carry_u", ch)])

        def gproj(ch, ns, cx, st):
            Gv, Gr = st["G"]
            wg, wgres = cx["wg"]
            pb, pres = proj(wg, wgres, H, o_H, KC, ns)
            S.op("dve", lambda e, Gv=Gv, pb=pb, ch=ch: e.tensor_scalar(out=Gv, in0=pb, scalar1=vcol(l, "b_in", c.GT0 + ch), scalar2=None, op0=ALU.add),
                 reads=pres + VR(l), writes=Gr)

        def conv(ch, ns, cx, st):
            ub, dg, ubr = cx["ub"], cx["dg"], cx["ubr"]
            UCB, UCBr = st["UCB"]
            o = 4 + ns * 512
            ubrd = ubr[ns] + ([(cx["ubname"],)] if ns == 0 else ubr[ns - 1])
            pcv, pcvres = pbank()
            for k in range(4):
                j = 3 - k
                S.op("pe", lambda e, pcv=pcv, dg=dg, ub=ub, k=k, j=j, o=o: e.matmul(pcv, lhsT=dg[:, k, :], rhs=ub[:, o - j:o - j + 512],
                                                                               start=(k == 0), stop=(k == 3)),
                     reads=[("dg", ch % 2, k)] + ubrd, writes=pcvres, inc=(k == 3))
            S.op("dve", lambda e, UCB=UCB, pcv=pcv, ch=ch: e.tensor_scalar(out=UCB, in0=pcv, scalar1=vcol(l, "cb", ch), scalar2=None, op0=ALU.add),
                 reads=pcvres + VR(l), writes=UCBr)

        def gates(ch, ns, cx, st):
            UCB, UCBr = st["UCB"]
            wa_c, wares = cx["wa"]
            wx_c, wxres = cx["wx"]
            pa, pares = pbank(fixed=4 + 2 * (ns % 2))
            S.op("pe", lambda e, pa=pa, wa_c=wa_c, UCB=UCB: e.matmul(pa, lhsT=wa_c[:, 0, :], rhs=UCB, start=True, stop=True),
                 reads=wares + UCBr, writes=pares)
            px, pxres = pbank(fixed=5 + 2 * (ns % 2))
            S.op("pe", lambda e, px=px, wx_c=wx_c, UCB=UCB: e.matmul(px, lhsT=wx_c[:, 0, :], rhs=UCB, start=True, stop=True),
                 reads=wxres + UCBr, writes=pxres)
            return (pa, pares, px, pxres)

        assert NS == 2
        cx = pre(0)
        uproj(0, 0, cx)
        uproj(0, 1, cx)
        for ch in range(KC):
            pre_b(ch, cx)
            nxt = pre(ch + 1) if ch + 1 < KC else None
            sets = [lru_sets[2 * (ch % 3) + ns] for ns in range(NS)]
            banks = [None, None]
            if not state_only:
                gproj(ch, 0, cx, sets[0])
            conv(ch, 0, cx, sets[0])
            if pend["q"] is not None:
                pend["q"]()
            if pend["dve2"] is not None:
                pend["dve2"](0)
            if not state_only:
                gproj(ch, 1, cx, sets[1])
            elif nxt is not None:
                uproj(ch + 1, 0, nxt)
            banks[0] = gates(ch, 0, cx, sets[0])
            if pend["p2"] is not None:
                pend["p2"]()
            conv(ch, 1, cx, sets[1])
            if pend["dve2"] is not None:
                pend["dve2"](1)
            if nxt is not None:
                uproj(ch + 1, 0 if not state_only else 1, nxt)
            banks[1] = gates(ch, 1, cx, sets[1])
            if nxt is not None and not state_only:
                uproj(ch + 1, 1, nxt)
            cx = nxt
            for ns in range(NS):
                st = sets[ns]
                pa, pares, px, pxres = banks[ns]
                R1, R1r = st["R1"]; I1, I1r = st["I1"]
                act(R1, pa, AF.Tanh, pares + DR(l, "hba"), R1r, bias=dcol(l, "hba", ch), scale=0.5)
                act(I1, px, AF.Tanh, pxres + DR(l, "hbx"), I1r, bias=dcol(l, "hbx", ch), scale=0.5)
            for ns in range(NS):
                st = sets[ns]
                R1, R1r = st["R1"]; A2, A2r = st["A2"]
                act(A2, R1, AF.Exp, R1r + DR(l, "nc8"), A2r, bias=dcol(l, "nc8", ch), scale=dcol(l, "nc8", ch))
                act(R1, R1, AF.Exp, R1r + DR(l, "hnc8"), R1r, bias=dcol(l, "hnc8", ch), scale=dcol(l, "hnc8", ch))
            if not state_only:
                for ns in range(NS):
                    st = sets[ns]
                    Gv, Gr = st["G"]; GL, GLr = st["GL"]
                    act(GL, Gv, AF.Square, Gr, GLr, scale=GC0)

            def q_dve(sets=sets):
                if state_only:
                    return
                for ns in range(NS):
                    Gv, Gr = sets[ns]["G"]; GL, GLr = sets[ns]["GL"]
                    S.op("dve", lambda e, GL=GL, Gv=Gv: e.scalar_tensor_tensor(out=GL, in0=GL, scalar=1.0, in1=Gv, op0=ALU.add, op1=ALU.mult),
                         reads=GLr + Gr, writes=GLr)

            def part2(sets=sets):
                if not state_only:
                    for ns in range(NS):
                        GL, GLr = sets[ns]["GL"]
                        act(GL, GL, AF.Tanh, GLr, GLr, scale=GC1)
                for ns in range(NS):
                    A2, A2r = sets[ns]["A2"]
                    act(A2, A2, AF.Sqrt, A2r + [("one",)], A2r, bias=ONE_AP[:, 0:1], scale=-1.0)

            def back_dve(ns, ch=ch, sets=sets):
                st = sets[ns]
                Gv, Gr = st["G"]; R1, R1r = st["R1"]; A2, A2r = st["A2"]; UCB, UCBr = st["UCB"]
                I1, I1r = st["I1"]; HL, HLr = st["HL"]; GL, GLr = st["GL"]
                S.op("dve", lambda e: e.scalar_tensor_tensor(out=I1, in0=I1, scalar=1.0, in1=UCB, op0=ALU.add, op1=ALU.mult),
                     reads=I1r + UCBr, writes=I1r)
                S.op("dve", lambda e: e.scalar_tensor_tensor(out=I1, in0=I1, scalar=0.5, in1=A2, op0=ALU.mult, op1=ALU.mult),
                     reads=I1r + A2r, writes=I1r)
                if ns == 0:
                    init, initr = lstate[:, ch:ch + 1], [("lstate", ch)]
                else:
                    init, initr = sets[ns - 1]["HL"][0][:, 511:512], sets[ns - 1]["HL"][1]
                S.op("dve", lambda e: e.tensor_tensor_scan(out=HL, data0=R1, data1=I1, initial=init, op0=ALU.mult, op1=ALU.add),
                     reads=R1r + I1r + initr, writes=HLr)
                if ns == NS - 1:
                    S.op("dve", lambda e: e.tensor_copy(out=lstate[:, ch:ch + 1], in_=HL[:, 511:512]), reads=HLr, writes=[("lstate", ch)])
                if not state_only:
                    S.op("dve", lambda e: e.scalar_tensor_tensor(out=GL, in0=GL, scalar=1.0, in1=Gv, op0=ALU.add, op1=ALU.mult),
                         reads=GLr + Gr, writes=GLr)
                    S.op("dve", lambda e: e.scalar_tensor_tensor(out=YL[:, ch, ns * 512:(ns + 1) * 512], in0=HL, scalar=0.5, in1=GL,
                                                                 op0=ALU.mult, op1=ALU.mult),
                         reads=HLr + GLr, writes=ares(o_YL, ch, ns * 512, (ns + 1) * 512))

            pend["dve2"] = pend["dve1"]
            pend["dve1"] = back_dve
            pend["q"] = q_dve
            pend["p2"] = part2
        if pend["q"] is not None:
            pend["q"]()
        if pend["p2"] is not None:
            pend["p2"]()
        for key in ("dve2", "dve1"):
            if pend[key] is not None:
                pend[key](0)
                pend[key](1)
        st_ps["n"] = 8

    def kv_halo(l):
        VTl, VTlr = lru_sets[0]["UCB"]
        for g in range(NKV):
            w, wres = wload(w_in[l, c.K0 + g, :, :], KC)
            pb, pres = pbank()
            mm_acc(pb[:, 0:128], pres, [(w[:, k, :], H[:, k, T - 128:T], wres + ares(o_H, k, T - 128, T)) for k in range(KC)])
            act(carry_k[:, g, :], pb[:, 0:128], AF.Identity, pres + VR(l), [("carry_k", g)], bias=vcol(l, "b_in", c.K0 + g))
            w, wres = wload(w_in[l, c.V0 + g, :, :], KC)
            pb, pres = pbank()
            mm_acc(pb[:, 0:128], pres, [(w[:, k, :], H[:, k, T - 128:T], wres + ares(o_H, k, T - 128, T)) for k in range(KC)])
            act(VTl[:, 0:128], pb[:, 0:128], AF.Identity, pres + VR(l), VTlr, bias=vcol(l, "b_in", c.V0 + g))
            pb2, pres2 = pbank()
            pbb = pb2.bitcast(BF16)
            S.op("pe", lambda e, pbb=pbb, VTl=VTl: e.transpose(out=pbb[:, 0:128], in_=VTl[:, 0:128], identity=ident_bf[:]),
                 reads=VTlr + [("ident",)], writes=pres2)
            S.op("dve", lambda e, pbb=pbb, g=g: e.tensor_copy(out=carry_v[:, g, :], in_=pbb[:, 0:128]), reads=pres2, writes=[("carry_v", g)])

    def prepass_tile(l, tt):
        load_norm1(l, tt)
        lru_branch(l, True)
        if tt == NT - 1:
            kv_halo(l)

    def exchange_send(l):
        XS, XSr = sview(0, WX, F32)
        XR, XRr = sview(2 * WX, WX, F32)
        o1, o2, o3 = KC, 5 * KC, 5 * KC + NKV * 128
        S.op("dve", lambda e: e.tensor_copy(out=XS[:, 0:o1], in_=lstate[:]), reads=[("lstate", i) for i in range(KC)], writes=XSr)
        S.op("dve", lambda e: e.tensor_copy(out=XS[:, o1:o2], in_=carry_u[:].rearrange("p k f -> p (k f)")),
             reads=[("carry_u", i) for i in range(KC)], writes=XSr)
        S.op("dve", lambda e: e.tensor_copy(out=XS[:, o2:o3], in_=carry_k[:].rearrange("p g d -> p (g d)")),
             reads=[("carry_k", i) for i in range(NKV)], writes=XSr)
        S.op("dve", lambda e: e.tensor_copy(out=XS[:, o3:WX], in_=carry_v[:].rearrange("p g d -> p (g d)")),
             reads=[("carry_v", i) for i in range(NKV)], writes=XSr)
        S.dma("sp", "st_snd", lambda e: e.dma_start(out=snd[l][:, :], in_=XS), reads=XSr, writes=[("snd", l)])
        groups = [[2 * i, 2 * i + 1] for i in range(c.NCORES // 2)]
        S.dma("pool", "cc", lambda e: e.collective_compute("AllGather", ALU.bypass, replica_groups=groups,
                                                           ins=[snd[l].opt()], outs=[rcv[l].opt()]),
              reads=[("snd", l)], writes=[("rcv", l)], inc=1)

    def exchange_recv(l):
        XR, XRr = sview(2 * WX, WX, F32)
        o1, o2, o3 = KC, 5 * KC, 5 * KC + NKV * 128
        S.dma("sp", "ld_rcv", lambda e: e.dma_start(out=XR, in_=rcv[l][0:128, :]), reads=[("rcv", l)], writes=XRr)
        fl = flag[:, 0:1]
        S.op("dve", lambda e: e.tensor_scalar(out=lstate[:], in0=XR[:, 0:o1], scalar1=fl, scalar2=None, op0=ALU.mult),
             reads=XRr + [("flag",)], writes=[("lstate", i) for i in range(KC)])
        S.op("dve", lambda e: e.tensor_scalar(out=carry_u[:].rearrange("p k f -> p (k f)"), in0=XR[:, o1:o2], scalar1=fl, scalar2=None, op0=ALU.mult),
             reads=XRr + [("flag",)], writes=[("carry_u", i) for i in range(KC)])
        S.op("dve", lambda e: e.tensor_scalar(out=carry_k[:].rearrange("p g d -> p (g d)"), in0=XR[:, o2:o3], scalar1=fl, scalar2=None, op0=ALU.mult),
             reads=XRr + [("flag",)], writes=[("carry_k", i) for i in range(NKV)])
        S.op("dve", lambda e: e.tensor_scalar(out=carry_v[:].rearrange("p g d -> p (g d)"), in0=XR[:, o3:WX], scalar1=fl, scalar2=None, op0=ALU.mult),
             reads=XRr + [("flag",)], writes=[("carry_v", i) for i in range(NKV)])

    def layer_tile(l, tt, last_layer):
        tok0 = tt * T
        xsrc = xin if l == 0 else xs
        xsrc_name = "xin" if l == 0 else "xs"
        load_norm1(l, tt)
        if tt == 0 and c.SPLIT == 2:
            exchange_recv(l)
        lru_branch(l, False)

        sc = 1.0 / math.sqrt(128.0)
        aunit = 0
        for g in range(NKV):
            for j in range(G):
                m = c.Q0 + g * G + j
                w, wres = wload(w_in[l, m, :, :], KC)
                for ns in range(NS):
                    pb, pres = proj(w, wres, H, o_H, KC, ns)
                    act(QG[:, j, ns * 512:(ns + 1) * 512], pb, AF.Identity, pres + VR(l), qgr(j, ns * 512, (ns + 1) * 512), bias=vcol(l, "b_in", m))
            S.op("dve", lambda e, g=g: e.tensor_copy(out=KT[:, 0:128], in_=carry_k[:, g, :]), reads=[("carry_k", g)], writes=ktr(0, 128))
            w, wres = wload(w_in[l, c.K0 + g, :, :], KC)
            for ns in range(NS):
                pb, pres = proj(w, wres, H, o_H, KC, ns)
                act(KT[:, 128 + ns * 512:128 + (ns + 1) * 512], pb, AF.Identity, pres + VR(l), ktr(128 + ns * 512, 128 + (ns + 1) * 512), bias=vcol(l, "b_in", c.K0 + g))
            S.op("dve", lambda e, g=g: e.tensor_copy(out=carry_k[:, g, :], in_=KT[:, T:T + 128]), reads=ktr(T, T + 128), writes=[("carry_k", g)])
            w, wres = wload(w_in[l, c.V0 + g, :, :], KC)
            for ns in range(NS):
                pb, pres = proj(w, wres, H, o_H, KC, ns)
                act(VT[:, ns * 512:(ns + 1) * 512], pb, AF.Identity, pres + VR(l), vtr(ns * 512, (ns + 1) * 512), bias=vcol(l, "b_in", c.V0 + g))
            S.op("dve", lambda e, g=g: e.tensor_copy(out=VTOK[:, 0, :], in_=carry_v[:, g, :]), reads=[("carry_v", g)], writes=vkr(0, 1))
            for q4 in range(NB // 4):
                pb, pres = pbank()
                pbb = pb.bitcast(BF16)
                for i in range(4):
                    b = q4 * 4 + i
                    S.op("pe", lambda e, pbb=pbb, i=i, b=b: e.transpose(out=pbb[:, i * 128:(i + 1) * 128], in_=VT[:, b * 128:(b + 1) * 128], identity=ident_bf[:]),
                         reads=vtr(b * 128, (b + 1) * 128) + [("ident",)], writes=pres, inc=(i == 3))
                S.op("dve", lambda e, pbb=pbb, q4=q4: e.tensor_copy(out=VTOK[:, 1 + q4 * 4:1 + q4 * 4 + 4, :],
                                                                   in_=pbb[:, 0:512].rearrange("p (b d) -> p b d", d=128)),
                     reads=pres, writes=vkr(1 + q4 * 4, 5 + q4 * 4))
            S.op("dve", lambda e, g=g: e.tensor_copy(out=carry_v[:, g, :], in_=VTOK[:, NB, :]), reads=vkr(NB, NB + 1), writes=[("carry_v", g)])
            if last_layer and tt > 0:
                if g == 0:
                    final_norm_sub(tt - 1, 0)
                if g == NKV - 1:
                    final_norm_sub(tt - 1, 1)
            for b in range(NB):
                first = (tt == 0 and b == 0 and c.SPLIT == 1)
                mp = mprev0 if (tt == 0 and b == 0) else mprev
                st = att_sets[aunit % 2]
                aunit += 1
                ERP, ERPr = st["ERP"]; ERC, ERCr = st["ERC"]; EP, EPr = st["EP"]; EC, ECr = st["EC"]; DEN, DENr = st["DEN"]
                qrhs = QG[:, :, b * 128:(b + 1) * 128]
                qres = [r for j in range(G) for r in qgr(j, b * 128, (b + 1) * 128)]
                if not first:
                    pp, ppres = pbank()
                    S.op("pe", lambda e, pp=pp, b=b, qrhs=qrhs: e.matmul(pp.rearrange("p (g q) -> p g q", g=G), lhsT=KT[:, b * 128:(b + 1) * 128], rhs=qrhs, start=True, stop=True),
                         reads=qres + ktr(b * 128, (b + 1) * 128), writes=ppres)
                pc, pcres = pbank()
                S.op("pe", lambda e, pc=pc, b=b, qrhs=qrhs: e.matmul(pc.rearrange("p (g q) -> p g q", g=G), lhsT=KT[:, 128 + b * 128:128 + (b + 1) * 128], rhs=qrhs, start=True, stop=True),
                     reads=qres + ktr(128 + b * 128, 128 + (b + 1) * 128), writes=pcres)
                if not first:
                    act(ERP, pp, AF.Exp, ppres, ERPr, scale=sc)
                    S.op("dve", lambda e, EP=EP, ERP=ERP, mp=mp: e.tensor_tensor(out=EP, in0=ERP, in1=mp[:], op=ALU.mult), reads=ERPr + [("mprev",), ("mprev0",)], writes=EPr)
                act(ERC, pc, AF.Exp, pcres, ERCr, scale=sc)
                S.op("dve", lambda e, EC=EC, ERC=ERC: e.tensor_tensor(out=EC, in0=ERC, in1=mcur[:], op=ALU.mult), reads=ERCr + [("mcur",)], writes=ECr)
                pd, pdres = pbank()
                po, pores = pbank()
                if not first:
                    S.op("pe", lambda e, pd=pd, EP=EP: e.matmul(pd, lhsT=ones_bf[:], rhs=EP, start=True, stop=False), reads=EPr + [("ones",)], writes=pdres, inc=False)
                S.op("pe", lambda e, pd=pd, EC=EC, first=first: e.matmul(pd, lhsT=ones_bf[:], rhs=EC, start=first, stop=True), reads=ECr + [("ones",)], writes=pdres)
                if not first:
                    S.op("pe", lambda e, po=po, EP=EP, b=b: e.matmul(po, lhsT=VTOK[:, b, :], rhs=EP, start=True, stop=False), reads=EPr + vkr(b, b + 1), writes=pores, inc=False)
                S.op("pe", lambda e, po=po, EC=EC, b=b, first=first: e.matmul(po, lhsT=VTOK[:, b + 1, :], rhs=EC, start=first, stop=True), reads=ECr + vkr(b + 1, b + 2), writes=pores)
                for j in range(G):
                    S.op("dve", lambda e, DEN=DEN, pd=pd, j=j, g=g: e.tensor_scalar(out=DEN[:, j * 128:(j + 1) * 128], in0=pd[:, j * 128:(j + 1) * 128],
                                                                                 scalar1=dcol(l, "esk", g * G + j), scalar2=None, op0=ALU.add),
                         reads=pdres + DR(l, "esk"), writes=DENr)
                act(DEN, DEN, AF.Ln, DENr, DENr)
                act(DEN, DEN, AF.Exp, DENr, DENr, scale=-1.0)
                S.op("dve", lambda e, DEN=DEN, po=po, g=g, b=b: e.tensor_tensor(out=YA[:, g * G:(g + 1) * G, b * 128:(b + 1) * 128],
                                                                              in0=po.rearrange("p (g q) -> p g q", g=G),
                                                                              in1=DEN.rearrange("p (g q) -> p g q", g=G), op=ALU.mult),
                     reads=pores + DENr, writes=[r for j in range(G) for r in ares(o_YA, g * G + j, b * 128, (b + 1) * 128)])

        SA, SAr = lru_sets[0]["G"]; SB, SBr = lru_sets[0]["UC"]
        SA2, SA2r = lru_sets[1]["G"]; SB2, SB2r = lru_sets[1]["UC"]
        mu = 0
        for m in range(KC):
            ada_tick(1)
            wA, wAres = wload(w_in[l, c.GA0 + m, :, :], KC)
            wB, wBres = wload(w_in[l, c.GB0 + m, :, :], KC)
            wl, wlres = wload(w_lo[l, m, :, :], KC)
            wt, wtres = wload(w_ao[l, m, :, :], AWC)
            for ns in range(NS):
                sa, sar, sbb, sbr = (SA, SAr, SB, SBr) if mu % 2 == 0 else (SA2, SA2r, SB2, SB2r)
                mu += 1
                pA, pAres = proj(wA, wAres, H, o_H, KC, ns)
                pB, pBres = proj(wB, wBres, H, o_H, KC, ns)
                pL, pLres = proj(wl, wlres, YL, o_YL, KC, ns)
                pT, pTres = proj(wt, wtres, YA, o_YA, AWC, ns)
                act(sa, pA, AF.Sigmoid, pAres + VR(l), sar, bias=vcol(l, "b_in", c.GA0 + m))
                act(sbb, pB, AF.Sigmoid, pBres + VR(l), sbr, bias=vcol(l, "b_in", c.GB0 + m))
                S.op("dve", lambda e, sa=sa, pL=pL: e.tensor_tensor(out=sa, in0=sa, in1=pL, op=ALU.mult), reads=sar + pLres, writes=sar)
                S.op("dve", lambda e, sbb=sbb, pT=pT: e.tensor_tensor(out=sbb, in0=sbb, in1=pT, op=ALU.mult), reads=sbr + pTres, writes=sbr)
                S.op("dve", lambda e, sa=sa, sbb=sbb, m=m, ns=ns: e.tensor_tensor(out=MG[:, m, ns * 512:(ns + 1) * 512], in0=sa, in1=sbb, op=ALU.add),
                     reads=sar + sbr, writes=ares(o_MG, m, ns * 512, (ns + 1) * 512))

        ada_need(l, 2)
        for m in range(KC):
            wo, wores = wload(w_o[l, m, :, :], KC)
            for ns in range(NS):
                xi = (m * NS + ns) % 2
                xcb = XCB[xi]
                S.dma("sp", "ld_xc%d" % xi, lambda e, xcb=xcb, m=m, ns=ns: e.dma_start(out=xcb[:], in_=xsrc[:, m, tok0 + ns * 512:tok0 + (ns + 1) * 512]),
                      reads=[(xsrc_name, m, tt, ns)], writes=[("xcb", xi)])
                pO, pOres = proj(wo, wores, MG, o_MG, KC, ns)
                S.op("dve", lambda e, pO=pO, m=m, ns=ns, xcb=xcb: e.scalar_tensor_tensor(
                    out=X2[:, m, ns * 512:(ns + 1) * 512], in0=pO, scalar=dcol(l, "mod", 2 * KC + m), in1=xcb[:],
                    op0=ALU.mult, op1=ALU.add),
                     reads=pOres + DRm(l, 2) + [("xcb", xi)], writes=ares(o_H, m, ns * 512, (ns + 1) * 512, F32))
            S.dma("sp", "st_x%d" % (m % 4), lambda e, m=m: e.dma_start(out=xs[:, m, tok0:tok0 + T], in_=X2[:, m, :]),
                  reads=ares(o_H, m, 0, T, F32), writes=[("xs", m, tt, n_) for n_ in range(NS)])

        norm_mod(l, X2, o_H, H2, o_YA, lambda kc: dcol(l, "s2", kc), lambda kc: dcol(l, "mod", 3 * KC + kc), 3,
                 need=lambda: ada_need(l, 4))
        SG, SGr = lru_sets[0]["G"]; SG2, SG2r = lru_sets[1]["G"]
        fu = 0
        for j in range(FC):
            ada_tick(1)
            wg, wgres = wload(w_f1[l, j, :, :], KC)
            wu, wures = wload(w_f1[l, FC + j, :, :], KC)
            for ns in range(NS):
                sg, sgr = (SG, SGr) if fu % 2 == 0 else (SG2, SG2r)
                fu += 1
                pg, pgres = proj(wg, wgres, H2, o_YA, KC, ns)
                pu, pures = proj(wu, wures, H2, o_YA, KC, ns)
                act(sg, pg, AF.Silu, pgres, sgr)
                S.op("dve", lambda e, sg=sg, pu=pu, j=j, ns=ns: e.tensor_tensor(out=HID[:, j, ns * 512:(ns + 1) * 512], in0=sg, in1=pu, op=ALU.mult),
                     reads=sgr + pures, writes=ares(o_H, j, ns * 512, (ns + 1) * 512))
        ada_need(l, 5)
        pieces = []
        k0 = 0
        while k0 < FC:
            pieces.append((k0, min(16, FC - k0)))
            k0 += 16
        XO = [lru_sets[0]["R1"], lru_sets[0]["A2"], lru_sets[1]["R1"], lru_sets[1]["A2"]]
        xo_i = 0
        for m in range(KC):
            ws = []
            for (k0, kn) in pieces:
                ws.append(wload(w_f2[l, m, :, k0 * 128:(k0 + kn) * 128], kn))
            for ns in range(NS):
                t0, t1 = ns * 512, (ns + 1) * 512
                xi = (m * NS + ns) % 2
                xcb = XCB[xi]
                S.dma("sp", "ld_xc%d" % xi, lambda e, xcb=xcb, m=m, t0=t0, t1=t1: e.dma_start(out=xcb[:], in_=xs[:, m, tok0 + t0:tok0 + t1]),
                      reads=[("xs", m, tt, ns)], writes=[("xcb", xi)])
                pb, pres = pbank()
                pairs = []
                for (k0, kn), (w, wres) in zip(pieces, ws):
                    for k in range(kn):
                        pairs.append((w[:, k, :], HID[:, k0 + k, t0:t1], wres + ares(o_H, k0 + k, t0, t1)))
                mm_acc(pb, pres, pairs)
                xo, xor_ = XO[xo_i % 4]
                xo_i += 1
                S.op("dve", lambda e, pb=pb, m=m, xo=xo, xcb=xcb: e.scalar_tensor_tensor(
                    out=xo, in0=pb, scalar=dcol(l, "mod", 5 * KC + m), in1=xcb[:], op0=ALU.mult, op1=ALU.add),
                     reads=pres + DRm(l, 5) + [("xcb", xi)], writes=xor_)
                S.dma("sp", "st_x%d" % (xo_i % 4), lambda e, xo=xo, m=m, t0=t0, t1=t1: e.dma_start(out=xs[:, m, tok0 + t0:tok0 + t1], in_=xo),
                      reads=xor_, writes=[("xs", m, tt, ns)])

        if last_layer and tt == NT - 1:
            for ns in range(NS):
                final_norm_sub(tt, ns)

    XF = arena[:, o_MG:o_MG + KC * T].bitcast(F32).rearrange("p (c t) -> p c t", t=512)

    def xfres(kc):
        return blk_res("A", o_MG * 2 + kc * 2048, o_MG * 2 + (kc + 1) * 2048)

    def final_norm_sub(tt, ns):
        tok = tt * T + ns * 512
        for kc in range(KC):
            S.dma("sp", "ld_xf%d" % (kc % 4), lambda e, kc=kc: e.dma_start(out=XF[:, kc, :], in_=xs[:, kc, tok:tok + 512]),
                  reads=[("xs", kc, tt, ns)], writes=xfres(kc))
        pb, pres = pbank()
        for kc in range(KC):
            i = kc % 2
            sq = DG[i][:].rearrange("p a b -> p (a b)")
            sqr = [("dg", i, k) for k in range(4)]
            act(sq, XF[:, kc, :], AF.Square, xfres(kc), sqr)
            S.op("pe", lambda e, sq=sq, kc=kc, pb=pb: e.matmul(pb, lhsT=ones_bf[:], rhs=sq, start=(kc == 0), stop=(kc == KC - 1)),
                 reads=sqr + [("ones",)], writes=pres, inc=True)
        RSx, RIx = XCB[0][:], XCB[1][:]
        act(RSx, pb, AF.Sqrt, pres + [("eps",)], [("xcb", 0)], bias=EPS_AP[:, 0:1], scale=1.0 / D)
        S.op("dve", lambda e: e.reciprocal(out=RIx, in_=RSx), reads=[("xcb", 0)], writes=[("xcb", 1)])
        for kc in range(KC):
            S.op("dve", lambda e, kc=kc: e.scalar_tensor_tensor(out=XF[:, kc, :], in0=XF[:, kc, :], scalar=vcol(0, "fg", kc), in1=RIx,
                                                                op0=ALU.mult, op1=ALU.mult),
                 reads=xfres(kc) + [("xcb", 1)] + VR(0), writes=xfres(kc))
            S.dma("sp", "st_y%d" % (kc % 4), lambda e, kc=kc: e.dma_start(out=yout[:, kc, tok:tok + 512], in_=XF[:, kc, :]),
                  reads=xfres(kc), writes=[("y", kc, tt, ns)])

    ONE_AP = sb("one_ap", [128, 1], F32)
    S.op("dve", lambda e: e.memset(ONE_AP[:], 1.0), writes=[("one",)])

    for l in range(DEPTH):
        layer_setup(l)
    for l in range(DEPTH):
        layer_state_reset(l)
        if c.SPLIT == 2:
            for tt in range(NT):
                prepass_tile(l, tt)
            exchange_send(l)
        for tt in range(NT):
            layer_tile(l, tt, l == DEPTH - 1)
    S.final_wait("sp", ["st_y%d" % i for i in range(4)] + ["st_x%d" % i for i in range(4)])
    S.emit()
    es.close()
    return nc


def _tile_w(W, kcn=None):
    K, N = W.shape
    kc, mc = K // 128, N // 128
    return np.ascontiguousarray(W.reshape(kc, 128, mc, 128).transpose(2, 1, 0, 3).reshape(mc, 128, kc * 128))


def _pcol(v):
    return np.ascontiguousarray(v.reshape(-1, 128).T)


def prep_shared(cfg, inp):
    c = cfg
    DEPTH, KC = c.DEPTH, c.KC
    f = lambda a: np.asarray(a, dtype=np.float32)
    sh = {}
    sh["ada_w"] = np.stack([_tile_w(f(inp["ada_w"][l])) for l in range(DEPTH)])
    sh["w_in"] = np.stack([_tile_w(f(inp["w_in"][l])) for l in range(DEPTH)])
    sh["lru_wa"] = np.stack([np.ascontiguousarray(f(inp["lru_wa"][l]).transpose(1, 0, 2).reshape(128, KC * 128)) for l in range(DEPTH)])
    sh["lru_wx"] = np.stack([np.ascontiguousarray(f(inp["lru_wx"][l]).transpose(1, 0, 2).reshape(128, KC * 128)) for l in range(DEPTH)])
    sh["w_lo"] = np.stack([_tile_w(f(inp["w_lru_out"][l])) for l in range(DEPTH)])
    sh["w_ao"] = np.stack([_tile_w(f(inp["w_attn_out"][l])) for l in range(DEPTH)])
    sh["w_o"] = np.stack([_tile_w(f(inp["w_o"][l])) for l in range(DEPTH)])
    sh["w_f1"] = np.stack([_tile_w(f(inp["w_ffn_in"][l])) for l in range(DEPTH)])
    sh["w_f2"] = np.stack([_tile_w(f(inp["w_ffn_out"][l])) for l in range(DEPTH)])
    vecs = np.zeros((DEPTH, 128, c.NV), np.float32)
    for l in range(DEPTH):
        def put(nm, arr):
            o = c.VO[nm]
            vecs[l, :, o:o + arr.shape[1]] = arr
        put("ada_b", _pcol(f(inp["ada_b"][l])))
        put("n1g", _pcol(f(inp["norm1_g"][l])))
        put("n2g", _pcol(f(inp["norm2_g"][l])))
        put("b_in", _pcol(f(inp["b_in"][l])))
        cw = f(inp["conv_w"][l])
        put("cw", np.concatenate([_pcol(cw[k]) for k in range(4)], axis=1))
        put("cb", _pcol(f(inp["conv_b"][l])))
        put("ba", _pcol(f(inp["lru_ba"][l])))
        put("bx", _pcol(f(inp["lru_bx"][l])))
        put("lam", _pcol(f(inp["lru_lambda"][l])))
        put("sink", np.broadcast_to(f(inp["sinks"][l])[None, :], (128, c.NQ)))
        put("fg", _pcol(f(inp["final_g"])))
    sh["vecs"] = vecs
    return sh


def prep_core(cfg, inp, r):
    c = cfg
    b, half = r // c.SPLIT, r % c.SPLIT
    x = np.asarray(inp["x"][b], dtype=np.float32)[half * c.TOK:(half + 1) * c.TOK]
    xT = np.ascontiguousarray(x.T.reshape(c.KC, 128, c.TOK).transpose(1, 0, 2))
    cv = _pcol(np.asarray(inp["c"][b], dtype=np.float32))
    fl = np.full((128, 1), float(half), np.float32)
    return {"xin": xT, "cvec": cv, "flag": fl}


_NC_CACHE = {}


def run_cfg(cfg, inp):
    key = (cfg.D, cfg.NQ, cfg.NKV, cfg.FH, cfg.SEQ, cfg.DEPTH, cfg.T, cfg.SPLIT, cfg.BATCH)
    if key not in _NC_CACHE:
        _NC_CACHE[key] = build_program(cfg)
    nc = _NC_CACHE[key]
    sh = prep_shared(cfg, inp)
    in_maps = []
    for r in range(cfg.NCORES):
        m = dict(sh)
        m.update(prep_core(cfg, inp, r))
        in_maps.append(m)
    res = run_bass_kernel_spmd(nc, in_maps, core_ids=list(range(cfg.NCORES)))
    out = np.empty((cfg.BATCH, cfg.SEQ, cfg.D), np.float32)
    for r in range(cfg.NCORES):
        b, half = r // cfg.SPLIT, r % cfg.SPLIT
        y = np.asarray(res.results[r]["y"])
        out[b, half * cfg.TOK:(half + 1) * cfg.TOK, :] = y.transpose(1, 0, 2).reshape(cfg.D, cfg.TOK).T
    return out


def kernel(**inputs):
    cfg = Cfg()
    return run_cfg(cfg, inputs)
```
